# Optimizing a Trainium2 kernel written in Bass

```python
import math
import jax, jax.numpy as jnp
from jax import lax
import numpy as np

D_MODEL = 1024
BATCH = 16
SEQ = 2048
DEPTH = 1

HEAD_DIM = 64
SB_HEADS = 8
SG_GROUPS = 8
SB_WIDTH = SB_HEADS * HEAD_DIM
SG_WIDTH = SG_GROUPS * HEAD_DIM
MIX_WIDTH = SB_WIDTH + SG_WIDTH
IN_WIDTH = 3 * SB_WIDTH + 2 * SG_WIDTH
Q_BLOCK = 128
CHUNK = 128
D_FF = 4 * D_MODEL
EPS = 1e-6

kernel_name = "hybrid_stickbreak_spatialgate_block"


def rms_norm(x, g):
    x32 = x.astype(jnp.float32)
    r = x32 * lax.rsqrt(jnp.mean(x32 * x32, axis=-1, keepdims=True) + EPS)
    return (r * g.astype(jnp.float32)).astype(x.dtype)


def stick_breaking_attention(q, k, v):
    B, H, S, Dh = q.shape
    scale = 1.0 / math.sqrt(Dh)
    outs = []
    for i in range(S // Q_BLOCK):
        k_end = (i + 1) * Q_BLOCK
        qb = q[:, :, i * Q_BLOCK:k_end]
        kb = k[:, :, :k_end]
        vb = v[:, :, :k_end]
        z = jnp.einsum('bhtd,bhsd->bhts', qb, kb).astype(jnp.float32) * scale
        t_pos = i * Q_BLOCK + jnp.arange(Q_BLOCK)[:, None]
        s_pos = jnp.arange(k_end)[None, :]
        mask = s_pos < t_pos
        log1m = jnp.where(mask, jax.nn.log_sigmoid(-z), 0.0)
        excl = lax.cumsum(log1m, axis=3, reverse=True) - log1m
        log_a = jax.nn.log_sigmoid(z) + excl
        a = jnp.where(mask, jnp.exp(log_a), 0.0)
        outs.append(jnp.einsum('bhts,bhsd->bhtd', a.astype(vb.dtype), vb))
    return jnp.concatenate(outs, axis=2)


def spatial_gating(u, v, w_s, b_s, v_norm_g):
    B, S, G, Dh = v.shape
    v = rms_norm(v, v_norm_g)
    vc = v.reshape(B, S // CHUNK, CHUNK, G, Dh)
    causal = jnp.tril(jnp.ones((CHUNK, CHUNK), dtype=bool))
    w = jnp.where(causal[None], w_s, 0.0).astype(v.dtype)
    y = jnp.einsum('gts,bcsgd->bctgd', w, vc)
    y = y + jnp.transpose(b_s)[None, None, :, :, None].astype(v.dtype)
    return u * y.reshape(B, S, G, Dh)


def setup_inputs(seed: int = 0) -> dict:
    key = jax.random.key(seed)
    ks = jax.random.split(key, 20)
    f = jnp.float32
    n = lambda k, shape, s: jax.random.normal(k, shape, f) * s
    inp = {
        "x": n(ks[0], (BATCH, SEQ, D_MODEL), 1.0),
        "norm1_g": 1.0 + n(ks[1], (D_MODEL,), 0.02),
        "w_in": n(ks[2], (D_MODEL, IN_WIDTH), D_MODEL ** -0.5),
        "q_norm_g": 1.0 + n(ks[3], (HEAD_DIM,), 0.02),
        "k_norm_g": 1.0 + n(ks[4], (HEAD_DIM,), 0.02),
        "sg_v_norm_g": 1.0 + n(ks[5], (HEAD_DIM,), 0.02),
        "sg_w": n(ks[6], (SG_GROUPS, CHUNK, CHUNK), CHUNK ** -0.5),
        "sg_b": 1.0 + n(ks[7], (SG_GROUPS, CHUNK), 0.01),
        "sb_out_norm_g": 1.0 + n(ks[8], (HEAD_DIM,), 0.02),
        "sg_out_norm_g": 1.0 + n(ks[9], (HEAD_DIM,), 0.02),
        "w_out": n(ks[10], (MIX_WIDTH, D_MODEL), MIX_WIDTH ** -0.5),
        "norm2_g": 1.0 + n(ks[11], (D_MODEL,), 0.02),
        "w_ff1": n(ks[12], (D_MODEL, D_FF), D_MODEL ** -0.5),
        "w_ff2": n(ks[13], (D_FF, D_MODEL), D_FF ** -0.5),
    }
    return inp


def reference(x, norm1_g, w_in, q_norm_g, k_norm_g, sg_v_norm_g, sg_w, sg_b,
              sb_out_norm_g, sg_out_norm_g, w_out, norm2_g, w_ff1, w_ff2):
    B, S, _ = x.shape
    h = x
    for _layer in range(DEPTH):
        xn = rms_norm(h, norm1_g)
        proj = jnp.einsum('bsd,de->bse', xn, w_in)
        q, k, v, u_sg, v_sg = jnp.split(
            proj, np.cumsum([SB_WIDTH, SB_WIDTH, SB_WIDTH, SG_WIDTH]).tolist(), axis=-1)
        q = rms_norm(q.reshape(B, S, SB_HEADS, HEAD_DIM), q_norm_g)
        k = rms_norm(k.reshape(B, S, SB_HEADS, HEAD_DIM), k_norm_g)
        v = v.reshape(B, S, SB_HEADS, HEAD_DIM)
        o_sb = stick_breaking_attention(q.transpose(0, 2, 1, 3), k.transpose(0, 2, 1, 3),
                                        v.transpose(0, 2, 1, 3)).transpose(0, 2, 1, 3)
        o_sb = rms_norm(o_sb, sb_out_norm_g)
        o_sg = spatial_gating(u_sg.reshape(B, S, SG_GROUPS, HEAD_DIM),
                              v_sg.reshape(B, S, SG_GROUPS, HEAD_DIM),
                              sg_w, sg_b, sg_v_norm_g)
        o_sg = rms_norm(o_sg, sg_out_norm_g)
        mix = jnp.concatenate([o_sb.reshape(B, S, SB_WIDTH),
                               o_sg.reshape(B, S, SG_WIDTH)], axis=-1)
        h = h + jnp.einsum('bse,ed->bsd', mix, w_out)
        hn = rms_norm(h, norm2_g)
        a = jax.nn.relu(jnp.einsum('bsd,df->bsf', hn, w_ff1))
        h = h + jnp.einsum('bsf,fd->bsd', a * a, w_ff2)
    return h
```

```python
import numpy as np
from contextlib import ExitStack
import concourse.bass as bass
import concourse.mybir as mybir
from concourse.bass_utils import run_bass_kernel_spmd

F32 = mybir.dt.float32
BF16 = mybir.dt.bfloat16
U8 = mybir.dt.uint8
AF = mybir.ActivationFunctionType
ALU = mybir.AluOpType
AX = mybir.AxisListType

NCORES = 8
NB = 2
S = 2048
D = 1024
HD = 64
DFF = 4096
INW = 2560
EPS = 1e-6
NT = S // 128

ENGS = ("pe", "act", "dve", "pool", "sp")


class Op:
    __slots__ = ("eng", "fn", "deps", "needs_inc", "token", "dma", "idx")

    def __init__(self, eng, fn, dma):
        self.eng = eng
        self.fn = fn
        self.deps = []
        self.needs_inc = False
        self.token = None
        self.dma = dma


class Sched:
    def __init__(self):
        self.ops = {e: [] for e in ENGS}
        self.res = {}
        self.last_real = {e: None for e in ENGS}
        self.last_compute = {e: None for e in ENGS}
        self.dmas_since_barrier = []
        self.group_slots = set()

    def add(self, eng, fn, reads=(), writes=(), dma=None, group=False):
        op = Op(eng, fn, dma)
        deps = []
        for r in reads:
            st = self.res.get(r)
            if st is not None and st[0] is not None:
                deps.append(st[0])
        for w in writes:
            st = self.res.get(w)
            if st is not None:
                if st[0] is not None:
                    deps.append(st[0])
                deps.extend(st[1])
        seen = set()
        for d in deps:
            if id(d) in seen:
                continue
            seen.add(id(d))
            if d.eng == "pe" and eng == "pe" and d.dma is None:
                continue
            op.deps.append(d)
            d.needs_inc = True
        for r in reads:
            st = self.res.setdefault(r, [None, []])
            st[1].append(op)
        for w in writes:
            self.res[w] = [op, []]
        op.idx = len(self.ops[eng])
        self.ops[eng].append(op)
        if fn is not None:
            self.last_real[eng] = op
            if dma is None:
                self.last_compute[eng] = op
        if dma is not None:
            self.dmas_since_barrier.append(op)
            if group:
                self.group_slots.add(dma)
        return op

    def barrier(self):
        lasts = dict(self.last_compute)
        dmas = list(self.dmas_since_barrier)
        self.dmas_since_barrier = []
        for e in ENGS:
            op = Op(e, None, None)
            op.idx = len(self.ops[e])
            for e2 in ENGS:
                l = lasts[e2]
                if l is not None and e2 != e and l.dma is None:
                    op.deps.append(l)
                    l.needs_inc = True
            for d in dmas:
                op.deps.append(d)
            self.ops[e].append(op)

    def assign_tokens(self, eng_sems, dma_sems):
        slot_cnt = {}
        slot_ops = {}
        for e in ENGS:
            cnt = 0
            for op in self.ops[e]:
                if op.dma is not None:
                    c = slot_cnt.get(op.dma, 0) + 16
                    slot_cnt[op.dma] = c
                    op.token = (dma_sems[op.dma], c)
                    slot_ops.setdefault(op.dma, []).append(op)
                elif op.needs_inc:
                    cnt += 1
                    op.token = (eng_sems[e], cnt)
        for slot in self.group_slots:
            for op in slot_ops[slot]:
                op.token = (dma_sems[slot], slot_cnt[slot])

    def dma_slots(self):
        s = []
        for e in ENGS:
            for op in self.ops[e]:
                if op.dma is not None and op.dma not in s:
                    s.append(op.dma)
        return s

    def emit(self, eng, e):
        waited = {}
        for op in self.ops[eng]:
            need = {}
            for d in op.deps:
                sem, val = d.token
                k = id(sem)
                if k not in need or need[k][1] < val:
                    need[k] = (sem, val)
            for k, (sem, val) in need.items():
                if waited.get(k, 0) >= val:
                    continue
                e.wait_ge(sem, val)
                waited[k] = val
            if op.fn is not None:
                ins = op.fn(e)
                if op.dma is not None:
                    ins.then_inc(op.token[0], 16)
                elif op.needs_inc:
                    ins.then_inc(op.token[0], 1)


class Arena:
    def __init__(self, ap, limit):
        self.ap = ap
        self.limit = limit

    def view(self, off, nbytes, dt, pattern=None, **kw):
        assert off % 4 == 0 and off + nbytes <= self.limit, (off, nbytes, self.limit)
        v = self.ap[:, off:off + nbytes].bitcast(dt)
        if pattern is not None:
            v = v.rearrange(pattern, **kw)
        return v


def bc_last(ap2, n):
    return ap2.unsqueeze(2).to_broadcast([ap2.shape[0], ap2.shape[1], n])


def bc_mid(ap2, n):
    return ap2.unsqueeze(1).to_broadcast([ap2.shape[0], n, ap2.shape[1]])


def build_nc():
    nc = bass.Bass("TRN2", target_bir_lowering=False)
    dt = nc.dram_tensor
    x = dt("x", [NB, S, D], F32, kind="ExternalInput").ap()
    norm1_g = dt("norm1_g", [D], F32, kind="ExternalInput").ap()
    w_in = dt("w_in", [D, INW], F32, kind="ExternalInput").ap()
    q_norm_g = dt("q_norm_g", [HD], F32, kind="ExternalInput").ap()
    k_norm_g = dt("k_norm_g", [HD], F32, kind="ExternalInput").ap()
    sg_v_norm_g = dt("sg_v_norm_g", [HD], F32, kind="ExternalInput").ap()
    sg_w = dt("sg_w", [8, 128, 128], F32, kind="ExternalInput").ap()
    sg_b = dt("sg_b", [8, 128], F32, kind="ExternalInput").ap()
    sb_out_norm_g = dt("sb_out_norm_g", [HD], F32, kind="ExternalInput").ap()
    sg_out_norm_g = dt("sg_out_norm_g", [HD], F32, kind="ExternalInput").ap()
    w_out = dt("w_out", [D, D], F32, kind="ExternalInput").ap()
    norm2_g = dt("norm2_g", [D], F32, kind="ExternalInput").ap()
    w_ff1 = dt("w_ff1", [D, DFF], F32, kind="ExternalInput").ap()
    w_ff2 = dt("w_ff2", [DFF, D], F32, kind="ExternalInput").ap()
    out = dt("out", [NB, S, D], F32, kind="ExternalOutput").ap()
    scr1 = dt("scr1", [D, DFF], BF16).ap()
    scr2 = dt("scr2", [DFF, D], BF16).ap()

    SC = Sched()
    es = ExitStack()
    with es:
        ARENA_BYTES = 207 * 1024
        arena_t = es.enter_context(nc.sbuf_tensor("arena", [128, ARENA_BYTES], U8))
        ps_t = es.enter_context(nc.psum_tensor("ps", [128, 4096], F32))
        AR = Arena(arena_t, ARENA_BYTES)

        def bank(k, n=1):
            return ps_t[:, k * 512:(k + n) * 512]

        def bank_bf(k):
            return ps_t[:, k * 512:(k + 1) * 512].bitcast(BF16)

        off = [0]

        def take(nbytes):
            o = off[0]
            off[0] += (nbytes + 31) // 32 * 32
            return o

        KB = 1024
        o_ident32 = take(512)
        o_ident = take(256); o_negtri = take(256); o_negones = take(256); o_bdones = take(256); o_mask = take(256)
        o_wt = take(2 * KB); o_sgb = take(32); o_gains = take(64)
        o_g1t = take(32); o_g2t = take(32); o_gvbc = take(256)
        o_small = take(1 * KB)
        o_wout = take(16 * KB)
        o_mixt = take(32 * KB)
        base_state = off[0]
        o_qt = take(16 * KB); o_kt = take(16 * KB); o_vpad = take(32 * KB)
        base_ab = off[0]
        o_win = take(40 * KB)
        o_x = [take(4 * KB), take(4 * KB)]
        o_xb = take(2 * KB)
        o_xt = [take(2 * KB), take(2 * KB)]
        o_qf = take(2 * KB); o_kf = take(2 * KB); o_vsf = take(2 * KB); o_uf = [take(2 * KB), take(2 * KB), take(2 * KB)]
        o_sq = take(6 * KB)
        o_stgf = o_sq; o_stgb = o_sq + 4 * KB
        o_n3 = [take(3 * KB), take(3 * KB)]; o_on = [take(1 * KB), take(1 * KB)]
        o_junk = take(2 * KB)
        o_t1 = take(2 * KB)
        end_a = off[0]
        off[0] = base_ab
        o_e = [take(4 * KB), take(4 * KB)]
        o_l = [take(2 * KB), take(2 * KB), take(2 * KB)]
        o_ls = take(4 * KB)
        o_lsb = [take(2 * KB), take(2 * KB)]
        o_a = [take(2 * KB), take(2 * KB)]
        o_qp = [take(2 * KB), take(2 * KB)]
        o_osb = take(2 * KB); o_sqb = take(1 * KB); o_r = take(2 * KB)
        end_b = off[0]
        off[0] = base_state
        o_af = take(32 * KB); o_h1 = [take(16 * KB), take(16 * KB)]
        assert off[0] == o_win, (off[0], o_win)
        o_w1 = [take(8 * KB), take(8 * KB), take(8 * KB)]
        o_h1t = [take(8 * KB), take(8 * KB)]
        assert off[0] == o_win + 40 * KB
        o_xr = [take(4 * KB), take(4 * KB)]
        o_h1b = take(2 * KB)
        o_junkc = take(2 * KB)
        o_rl = [take(1 * KB), take(1 * KB)]
        o_w2 = [take(16 * KB), take(16 * KB)]
        end_c = off[0]
        assert max(end_a, end_b, end_c) <= ARENA_BYTES, (end_a, end_b, end_c)

        V = AR.view
        IDENT32 = V(o_ident32, 512, F32)
        IDENT = V(o_ident, 256, BF16); NEGTRI = V(o_negtri, 256, BF16); NEGONES = V(o_negones, 256, BF16)
        BDONES = V(o_bdones, 256, BF16); MASK = V(o_mask, 256, BF16)
        WT = V(o_wt, 2 * KB, BF16, "p (g t) -> p g t", g=8)
        SGB = V(o_sgb, 32, F32)
        GAINS = V(o_gains, 64, F32)
        G1T = V(o_g1t, 32, F32); G2T = V(o_g2t, 32, F32); GVBC = V(o_gvbc, 256, F32)
        SMALL = V(o_small, 1 * KB, F32)
        STGF = V(o_stgf, 4 * KB, F32, "p (g s) -> p g s", g=8)
        STGB = V(o_stgb, 2 * KB, BF16, "p (g s) -> p g s", g=8)
        WOUT = V(o_wout, 16 * KB, BF16, "p (c f) -> p c f", c=8)
        MIXT = V(o_mixt, 32 * KB, BF16, "p (c t) -> p c t", c=8)
        QT = V(o_qt, 16 * KB, BF16, "p (c t) -> p c t", c=4)
        KT = V(o_kt, 16 * KB, BF16, "p (c t) -> p c t", c=4)
        VPAD = V(o_vpad, 32 * KB, BF16, "p (i c h d) -> p i c h d", i=16, c=4, h=2)
        VPAD_FLAT = V(o_vpad, 32 * KB, BF16)
        WIN = V(o_win, 40 * KB, BF16, "p (c f) -> p c f", c=8)
        X = [V(o, 4 * KB, F32) for o in o_x]
        SQT = V(o_xb, 2 * KB, F32)
        XT = [V(o, 2 * KB, BF16, "p (c t) -> p c t", c=8) for o in o_xt]
        XTF = [V(o, 2 * KB, BF16) for o in o_xt]
        QF = V(o_qf, 2 * KB, F32); KF = V(o_kf, 2 * KB, F32); VSF = V(o_vsf, 2 * KB, F32); UF = [V(o, 2 * KB, F32) for o in o_uf]
        SQ3 = V(o_sq, 6 * KB, F32)
        SQ = V(o_sq, 2 * KB, F32)
        F3 = V(o_qf, 6 * KB, F32)
        N3 = [V(o, 3 * KB, BF16) for o in o_n3]
        JUNK = V(o_junk, 2 * KB, BF16)
        ON = [V(o, 1 * KB, BF16) for o in o_on]
        T1 = V(o_t1, 2 * KB, F32)
        E = [V(o, 4 * KB, F32, "p (h t) -> p h t", h=2) for o in o_e]
        L = [V(o, 2 * KB, BF16, "p (h t) -> p h t", h=2) for o in o_l]
        LS = V(o_ls, 4 * KB, F32, "p (h t) -> p h t", h=2)
        LSB = [V(o, 2 * KB, BF16, "p (h t) -> p h t", h=2) for o in o_lsb]
        A = [V(o, 2 * KB, BF16, "p (h t) -> p h t", h=2) for o in o_a]
        QP = [V(o, 2 * KB, BF16, "p (h t) -> p h t", h=2) for o in o_qp]
        OSB = V(o_osb, 2 * KB, F32); SQB = V(o_sqb, 1 * KB, BF16); R = V(o_r, 2 * KB, F32)
        AFF = V(o_af, 32 * KB, BF16, "p (c t) -> p c t", c=32)
        H1T = [V(o, 8 * KB, BF16, "p (c t) -> p c t", c=8) for o in o_h1t]
        H1 = [V(o, 16 * KB, F32, "p (s f) -> p s f", s=4) for o in o_h1]
        XR = [V(o, 4 * KB, F32) for o in o_xr]
        H1B = V(o_h1b, 2 * KB, BF16)
        JUNKC = V(o_junkc, 2 * KB, BF16)
        RL = [V(o, 1 * KB, BF16) for o in o_rl]
        W1 = [V(o, 8 * KB, BF16, "p (c f) -> p c f", c=8) for o in o_w1]
        W2 = [V(o, 16 * KB, BF16, "p (c f) -> p c f", c=32) for o in o_w2]

        sm = [0]

        def small(n):
            c = sm[0]
            sm[0] += n
            assert sm[0] <= 256
            return SMALL[:, c:c + n]

        SS1 = [small(1), small(1)]
        RT1 = [small(1), small(1)]
        RSTD1 = [small(1), small(1)]
        SSQ8 = small(8); RT8 = small(8); R8 = small(8)
        SSQ24 = small(24); RT24 = small(24); R24 = small(24)
        SS2 = small(4); RT2 = small(4); R2 = small(4); R2SQ = [small(4), small(4)]

        add = SC.add
        GAINS_R = [("GAINS", col, hh) for col in (0, 1, 3, 4) for hh in range(2)]
        WIN_R = [("WIN", c) for c in range(8)]
        WOUT_R = [("WOUT", c) for c in range(8)]
        SCR1_R = [("scr1", i) for i in range(4)]
        SCR2_R = [("scr2", i) for i in range(4)]

        def scr_casts():
            for i in range(4):
                add("pool", lambda e, i=i: e.dma_start(out=scr1[i * 256:(i + 1) * 256, :], in_=w_ff1[i * 256:(i + 1) * 256, :]),
                    writes=[("scr1", i)], dma="scr", group=True)
            for i in range(4):
                add("pool", lambda e, i=i: e.dma_start(out=scr2[i * 1024:(i + 1) * 1024, :], in_=w_ff2[i * 1024:(i + 1) * 1024, :]),
                    writes=[("scr2", i)], dma="scr", group=True)

        def wout_load():
            for c in range(8):
                add("pool", lambda e, c=c: e.dma_start(out=WOUT[:, c, :], in_=w_out[c * 128:(c + 1) * 128, :]),
                    writes=[("WOUT", c)], dma="wout", group=True)

        def setup():
            add("sp", lambda e: e.dma_start(out=G1T, in_=norm1_g.rearrange("(c p) -> p c", p=128), allow_slow_non_contiguous=True),
                writes=["G1T"], dma="const", group=True)
            add("sp", lambda e: e.dma_start(out=G2T, in_=norm2_g.rearrange("(c p) -> p c", p=128), allow_slow_non_contiguous=True),
                writes=["G2T"], dma="const", group=True)
            add("sp", lambda e: e.dma_start(out=GVBC, in_=sg_v_norm_g.partition_broadcast(128)), writes=["GVBC"], dma="const", group=True)
            add("sp", lambda e: e.dma_start(out=STGF, in_=sg_w.rearrange("g t s -> t g s")), writes=["SQ"], dma="const", group=True)
            add("sp", lambda e: e.dma_start(out=SGB, in_=sg_b.rearrange("g t -> t g"), allow_slow_non_contiguous=True),
                writes=["SGB"], dma="const", group=True)
            for col, gsrc in ((0, q_norm_g), (1, k_norm_g), (3, sb_out_norm_g), (4, sg_out_norm_g)):
                for hh in range(2):
                    add("sp", lambda e, col=col, gsrc=gsrc, hh=hh: e.dma_start(
                        out=GAINS[hh * 64:(hh + 1) * 64, col:col + 1], in_=gsrc.rearrange("(d o) -> d o", o=1)),
                        writes=[("GAINS", col, hh)], dma="const", group=True)
            add("pool", lambda e: e.memset(IDENT32, 1.0), writes=["IDENT32"])
            add("pool", lambda e: e.affine_select(out=IDENT32, in_=IDENT32, pattern=[[1, 128]], compare_op=ALU.is_equal, fill=0.0, base=0, channel_multiplier=-1),
                reads=["IDENT32"], writes=["IDENT32"])
            add("pool", lambda e: e.memset(IDENT, 1.0), writes=["IDENT"])
            add("pool", lambda e: e.affine_select(out=IDENT, in_=IDENT, pattern=[[1, 128]], compare_op=ALU.is_equal, fill=0.0, base=0, channel_multiplier=-1),
                reads=["IDENT"], writes=["IDENT"])
            win_load(0)
            add("pool", lambda e: e.memset(NEGTRI, -1.0), writes=["NEGTRI"])
            add("pool", lambda e: e.affine_select(out=NEGTRI, in_=NEGTRI, pattern=[[-1, 128]], compare_op=ALU.is_ge, fill=0.0, base=0, channel_multiplier=1),
                reads=["NEGTRI"], writes=["NEGTRI"])
            add("pool", lambda e: e.memset(NEGONES, -1.0), writes=["NEGONES"])
            add("pool", lambda e: e.memset(BDONES, 0.0), writes=["BDONES"])
            add("pool", lambda e: e.memset(BDONES[0:64, 0:64], 1.0), reads=["BDONES"], writes=["BDONES"])
            add("pool", lambda e: e.memset(BDONES[64:128, 64:128], 1.0), reads=["BDONES"], writes=["BDONES"])
            add("pool", lambda e: e.memset(MASK, 1.0), writes=["MASK"])
            add("pool", lambda e: e.affine_select(out=MASK, in_=MASK, pattern=[[1, 128]], compare_op=ALU.is_gt, fill=0.0, base=0, channel_multiplier=-1),
                reads=["MASK"], writes=["MASK"])
            add("dve", lambda e: e.tensor_tensor(out=GAINS[:, 2:3], in0=GAINS[:, 0:1], in1=GAINS[:, 1:2], op=ALU.mult),
                reads=GAINS_R, writes=["GQK0"])
            add("dve", lambda e: e.tensor_scalar(out=GAINS[:, 2:3], in0=GAINS[:, 2:3], scalar1=0.125, scalar2=None, op0=ALU.mult),
                reads=["GQK0"], writes=["GQK"])
            add("pool", lambda e: e.tensor_copy(out=STGB, in_=STGF), reads=["SQ"], writes=["SQ"])
            tp = bank_bf(7).rearrange("p (g t) -> p g t", g=8)
            def tr_w(e):
                ins = None
                for g in range(8):
                    ins = e.transpose(out=tp[:, g, :], in_=STGB[:, g, :], identity=IDENT)
                return ins
            add("pe", tr_w, reads=["SQ", "IDENT"], writes=["B7"])
            add("dve", lambda e: e.tensor_copy(out=WT, in_=tp), reads=["B7"], writes=["WT0"])
            add("pool", lambda e: e.affine_select(out=WT, in_=WT, pattern=[[0, 8], [1, 128]], compare_op=ALU.is_ge, fill=0.0,
                                                  base=0, channel_multiplier=-1), reads=["WT0"], writes=["WT"])

        def win_load(b, alias=False):
            for c in range(8):
                wr = [("WIN", c)]
                if alias and c == 0:
                    wr += [("W1", 0), ("W1", 1), ("W1", 2), ("H1T", 0), ("H1T", 1)]
                add("pool", lambda e, c=c: e.dma_start(out=WIN[:, c, :], in_=w_in[c * 128:(c + 1) * 128, :]),
                    writes=wr, dma="win%d" % b, group=True)

        def phase_a(b):
            if b == 0:
                wout_load()
            add("pool", lambda e: e.memset(VPAD_FLAT, 0.0), writes=["VPAD"])
            P = [bank(j) for j in range(5)]
            TPX = bank_bf(5).rearrange("p (c t) -> p c t", c=8)
            TPXA = bank(5).rearrange("p (c t) -> p c t", c=4)
            TPXB = bank(7).rearrange("p (c t) -> p c t", c=4)
            Y = bank(2)
            TQK = bank_bf(6).rearrange("p (c t) -> p c t", c=8)
            TS = bank_bf(7)[:, 0:512].rearrange("p (c t) -> p c t", c=4)
            g8 = lambda ap: ap.rearrange("p (g d) -> p g d", d=64)

            def load_x(i):
                r = i % 2
                add("sp", lambda e: e.dma_start(out=X[r], in_=x[b, i * 128:(i + 1) * 128, :]), writes=[("X", r)], dma="X%d" % r)

            def front0a(i):
                r = i % 2
                add("act", lambda e: e.activation(out=JUNK, in_=X[r], func=AF.Square, accum_out=SS1[r]),
                    reads=[("X", r)], writes=[("SS1", r)])

            def front0b(i):
                r = i % 2
                add("act", lambda e: e.activation(out=RT1[r], in_=SS1[r], func=AF.Sqrt, scale=1.0 / D, bias=EPS),
                    reads=[("SS1", r)], writes=[("RT1", r)])
                add("dve", lambda e: e.reciprocal(out=RSTD1[r], in_=RT1[r]), reads=[("RT1", r)], writes=[("RSTD1", r)])

            def front0(i):
                front0a(i)
                front0b(i)

            def tp(i):
                r = i % 2
                def tra(e):
                    ins = None
                    for c in range(4):
                        ins = e.transpose(out=TPXA[:, c, :], in_=X[r][:, c * 128:(c + 1) * 128], identity=IDENT32)
                    return ins
                def trb(e):
                    ins = None
                    for c in range(4):
                        ins = e.transpose(out=TPXB[:, c, :], in_=X[r][:, (4 + c) * 128:(5 + c) * 128], identity=IDENT32)
                    return ins
                add("pe", tra, reads=[("X", r), "IDENT32"], writes=["B5"])
                add("pe", trb, reads=[("X", r), "IDENT32"], writes=["B7"])
                add("dve", lambda e: e.tensor_tensor(out=XT[r][:, 0:4, :], in0=TPXA, in1=bc_last(G1T[:, 0:4], 128), op=ALU.mult),
                    reads=["B5", "G1T"], writes=[("XT", r, 0)])
                add("dve", lambda e: e.tensor_tensor(out=XT[r][:, 4:8, :], in0=TPXB, in1=bc_last(G1T[:, 4:8], 128), op=ALU.mult),
                    reads=["B7", "G1T"], writes=[("XT", r, 1)])

            def proj(i):
                r = i % 2
                for j in range(5):
                    def mm(e, j=j):
                        ins = None
                        for c in range(8):
                            ins = e.matmul(P[j], lhsT=XT[r][:, c, :], rhs=WIN[:, c, j * 512:(j + 1) * 512], start=(c == 0), stop=(c == 7))
                        return ins
                    add("pe", mm, reads=[("XT", r, 0), ("XT", r, 1)] + WIN_R, writes=[("P", j)])

            def front2(i):
                r = i % 2
                sc = RSTD1[r]
                p2 = P[2].rearrange("p (c h d) -> p c h d", c=4, h=2)
                for hh in range(2):
                    add("act", lambda e, hh=hh: e.activation(out=VPAD[:, i, :, hh, hh * 64:(hh + 1) * 64], in_=p2[:, :, hh, :], func=AF.Copy, scale=sc),
                        reads=[("P", 2), ("RSTD1", r), "VPAD"], writes=[("VPADW", hh)])
                add("act", lambda e: e.activation(out=QF, in_=P[0], func=AF.Copy, scale=sc), reads=[("P", 0), ("RSTD1", r)], writes=["QF"])
                add("act", lambda e: e.activation(out=KF, in_=P[1], func=AF.Copy, scale=sc), reads=[("P", 1), ("RSTD1", r)], writes=["KF"])
                add("act", lambda e: e.activation(out=VSF, in_=P[4], func=AF.Copy, scale=sc), reads=[("P", 4), ("RSTD1", r)], writes=["VSF"])
                u3 = i % 3
                add("act", lambda e: e.activation(out=UF[u3], in_=P[3], func=AF.Copy, scale=sc), reads=[("P", 3), ("RSTD1", r)], writes=[("UF", u3)])

            def early_a1(i):
                add("act", lambda e: e.activation(out=SQ3, in_=F3, func=AF.Square), reads=["QF", "KF", "VSF"], writes=["SQ"])
                add("dve", lambda e: e.tensor_reduce(out=SSQ24, in_=g8(SQ3), axis=AX.X, op=ALU.add), reads=["SQ"], writes=["SSQ24"])

            def early_a2(i):
                add("act", lambda e: e.activation(out=RT24, in_=SSQ24, func=AF.Sqrt, scale=1.0 / HD, bias=EPS), reads=["SSQ24"], writes=["RT24"])
                add("dve", lambda e: e.reciprocal(out=R24, in_=RT24), reads=["RT24"], writes=["R24"])
                nb = i % 2
                add("dve", lambda e: e.tensor_tensor(out=g8(N3[nb]), in0=g8(F3), in1=bc_last(R24, 64), op=ALU.mult),
                    reads=["QF", "KF", "VSF", "R24"], writes=[("N3", nb)])

            def early_b(i):
                tok = slice(i * 128, (i + 1) * 128)
                nb = i % 2
                def trqk(e):
                    ins = None
                    for p in range(8):
                        ins = e.transpose(out=TQK[:, p, :], in_=N3[nb][:, p * 128:(p + 1) * 128], identity=IDENT)
                    return ins
                add("pe", trqk, reads=[("N3", nb), "IDENT"], writes=["B6"])
                add("dve", lambda e: e.tensor_copy(out=QT[:, :, tok], in_=TQK[:, 0:4, :]), reads=["B6"], writes=["QT", "B6"])
                add("act", lambda e: e.activation(out=KT[:, :, tok], in_=TQK[:, 4:8, :], func=AF.Copy, scale=GAINS[:, 2:3]),
                    reads=["B6", "GQK"], writes=["KT", "B6"])

            def late1a(i):
                nb = i % 2
                u3 = i % 3
                def sg(e):
                    ins = None
                    for g in range(8):
                        ins = e.matmul(Y[:, g * 64:(g + 1) * 64], lhsT=WT[:, g, :], rhs=N3[nb][:, 1024 + g * 64:1024 + (g + 1) * 64], start=True, stop=True)
                    return ins
                add("pe", sg, reads=[("N3", nb), "WT"], writes=[("P", 2)])
                add("dve", lambda e: e.tensor_tensor(out=g8(T1), in0=g8(Y), in1=bc_mid(GVBC, 8), op=ALU.mult), reads=[("P", 2), "GVBC"], writes=["T1"])
                add("pool", lambda e: e.tensor_tensor(out=g8(T1), in0=g8(T1), in1=bc_last(SGB, 64), op=ALU.add), reads=["T1", "SGB"], writes=["T1"])
                add("pool", lambda e: e.tensor_tensor(out=T1, in0=T1, in1=UF[u3], op=ALU.mult), reads=["T1", ("UF", u3)], writes=["T1"])

            def late1b(i):
                add("act", lambda e: e.activation(out=SQT, in_=T1, func=AF.Square), reads=["T1"], writes=["SQT"])
                add("dve", lambda e: e.tensor_reduce(out=SSQ8, in_=g8(SQT), axis=AX.X, op=ALU.add), reads=["SQT"], writes=["SSQ8"])

            def late1c(i):
                r = i % 2
                add("act", lambda e: e.activation(out=RT8, in_=SSQ8, func=AF.Sqrt, scale=1.0 / HD, bias=EPS), reads=["SSQ8"], writes=["RT8"])
                add("dve", lambda e: e.reciprocal(out=R8, in_=RT8), reads=["RT8"], writes=["R8"])
                add("dve", lambda e: e.tensor_tensor(out=g8(ON[r]), in0=g8(T1), in1=bc_last(R8, 64), op=ALU.mult), reads=["T1", "R8"], writes=[("ON", r)])

            def late2(i):
                r = i % 2
                tok = slice(i * 128, (i + 1) * 128)
                def trs(e):
                    ins = None
                    for p in range(4):
                        ins = e.transpose(out=TS[:, p, :], in_=ON[r][:, p * 128:(p + 1) * 128], identity=IDENT)
                    return ins
                add("pe", trs, reads=[("ON", r), "IDENT"], writes=["B7"])
                add("act", lambda e: e.activation(out=MIXT[:, 4:8, tok], in_=TS, func=AF.Copy, scale=GAINS[:, 4:5]),
                    reads=["B7"] + GAINS_R, writes=["MIXT"])

            load_x(0)
            load_x(1)
            front0(0)
            tp(0)
            load_x(2)
            for i in range(NT + 3):
                if i < NT:
                    proj(i)
                    front2(i)
                if i + 1 < NT:
                    tp(i + 1)
                if 0 <= i - 2 < NT:
                    early_b(i - 2)
                    late1a(i - 2)
                if i < NT:
                    early_a1(i)
                if i + 1 < NT:
                    front0a(i + 1)
                    if i + 3 < NT:
                        load_x(i + 3)
                if 0 <= i - 2 < NT:
                    late1b(i - 2)
                if i < NT:
                    early_a2(i)
                if i + 1 < NT:
                    front0b(i + 1)
                if 0 <= i - 2 < NT:
                    late1c(i - 2)
                if 0 <= i - 3 < NT:
                    late2(i - 3)

        def phase_b(b):
            if b == 0:
                scr_casts()
            for k in range(2):
                add("pool", lambda e, k=k: e.memset(QP[k], 0.0), writes=[("QP", k)])
            steps = []
            for p in range(4):
                for g in range(4):
                    top = 4 * g + 3
                    for kb in range(top, -1, -1):
                        steps.append((p, g, kb))
            ns = len(steps)
            Z = [ps_t[:, zb * 1024:(zb + 1) * 1024].rearrange("p (h t) -> p h t", h=2) for zb in range(3)]
            OB = bank(6)
            SSB = bank(7)

            def info(n):
                p, g, kb = steps[n]
                j = kb - 4 * g
                c0 = max(0, j) * 128
                return p, g, kb, j, c0

            def cols_of(n):
                return slice(info(n)[4], 512)

            def st_z(n):
                p, g, kb, j, c0 = info(n)
                cs = slice(c0, 512)
                qb = g % 2
                if kb == 4 * g + 3:
                    t0 = g * 512
                    add("pool", lambda e: e.tensor_copy(out=QP[qb][0:64, 0, :], in_=QT[0:64, p, t0:t0 + 512]), reads=["QT"], writes=[("QP", qb)])
                    add("pool", lambda e: e.tensor_copy(out=QP[qb][64:128, 1, :], in_=QT[64:128, p, t0:t0 + 512]), reads=["QT", ("QP", qb)], writes=[("QP", qb)])
                zb = n % 3
                def mm(e):
                    ins = None
                    for hh in range(2):
                        ins = e.matmul(Z[zb][:, hh, cs], lhsT=KT[:, p, kb * 128:(kb + 1) * 128], rhs=QP[qb][:, hh, cs], start=True, stop=True)
                    return ins
                add("pe", mm, reads=["KT", ("QP", qb)], writes=[("Z", zb)])

            def st_exp1(n):
                cs = cols_of(n)
                zb = n % 3; eb = n % 2
                add("act", lambda e: e.activation(out=E[eb][:, :, cs], in_=Z[zb][:, :, cs], func=AF.Exp), reads=[("Z", zb)], writes=[("E", eb)])

            def st_ln(n):
                p, g, kb, j, c0 = info(n)
                cs = slice(c0, 512)
                eb = n % 2; lb = n % 3
                add("act", lambda e: e.activation(out=L[lb][:, :, cs], in_=E[eb][:, :, cs], func=AF.Ln, bias=1.0), reads=[("E", eb)], writes=[("L", lb)])
                if j >= 0:
                    dc = slice(c0, c0 + 128)
                    add("dve", lambda e: e.tensor_tensor(out=L[lb][:, :, dc], in0=L[lb][:, :, dc], in1=bc_mid(MASK, 2), op=ALU.mult),
                        reads=[("L", lb), "MASK"], writes=[("L", lb)])

            def st_carry(n):
                p, g, kb, j, c0 = info(n)
                cs = slice(c0, 512)
                lb = n % 3
                if kb >= 2:
                    if kb == 4 * g + 3:
                        add("dve", lambda e: e.memset(LS, 0.0), writes=["LS", "LS1"])
                    add("pool", lambda e: e.tensor_tensor(out=LS[:, :, cs], in0=LS[:, :, cs], in1=L[lb][:, :, cs], op=ALU.add),
                        reads=["LS", ("L", lb)], writes=["LS"])

            def st_cast(n):
                p, g, kb, j, c0 = info(n)
                if kb >= 2:
                    ncs = cols_of(n + 2)
                    sb = n % 2
                    add("dve", lambda e: e.tensor_copy(out=LSB[sb][:, :, ncs], in_=LS[:, :, ncs]), reads=["LS", "LS1"], writes=[("LSB", sb)])

            def st_pe2(n):
                p, g, kb, j, c0 = info(n)
                cs = slice(c0, 512)
                zb = n % 3; lb = n % 3; sb = n % 2
                plb = (n - 1) % 3
                m = 4 * g + 3 - kb
                pcs = cols_of(n - 1) if m >= 1 else None
                def mm(e):
                    ins = None
                    for hh in range(2):
                        if m >= 2:
                            e.matmul(Z[zb][:, hh, cs], lhsT=NEGONES, rhs=LSB[sb][:, hh, cs], start=False, stop=False, skip_group_check=True)
                        if m >= 1:
                            e.matmul(Z[zb][:, hh, pcs], lhsT=NEGONES, rhs=L[plb][:, hh, pcs], start=False, stop=False, skip_group_check=True)
                        ins = e.matmul(Z[zb][:, hh, cs], lhsT=NEGTRI, rhs=L[lb][:, hh, cs], start=False, stop=True, skip_group_check=True)
                    return ins
                rd = [("L", lb), "NEGTRI", "NEGONES"]
                if m >= 1:
                    rd.append(("L", plb))
                if m >= 2:
                    rd.append(("LSB", sb))
                add("pe", mm, reads=rd, writes=[("Z", zb)])
                st_carry(n)

            def st_exp3(n):
                p, g, kb, j, c0 = info(n)
                cs = slice(c0, 512)
                zb = n % 3; ab = n % 2
                add("act", lambda e: e.activation(out=A[ab][:, :, cs], in_=Z[zb][:, :, cs], func=AF.Exp), reads=[("Z", zb)], writes=[("A", ab)])
                if j >= 0:
                    dc = slice(c0, c0 + 128)
                    add("dve", lambda e: e.tensor_tensor(out=A[ab][:, :, dc], in0=A[ab][:, :, dc], in1=bc_mid(MASK, 2), op=ALU.mult),
                        reads=[("A", ab), "MASK"], writes=[("A", ab)])

            def st_av(n):
                p, g, kb, j, c0 = info(n)
                cs = slice(c0, 512)
                ab = n % 2
                first = (kb == 4 * g + 3)
                last = (kb == 0)
                def mm(e):
                    ins = None
                    for hh in range(2):
                        ins = e.matmul(OB[:, cs], lhsT=VPAD[:, kb, p, hh, :], rhs=A[ab][:, hh, cs], start=(first and hh == 0),
                                       stop=(last and hh == 1), skip_group_check=True)
                    return ins
                add("pe", mm, reads=["VPAD", ("A", ab)], writes=["OB"])
                if last:
                    t0 = g * 512
                    add("dve", lambda e: e.tensor_copy(out=OSB, in_=OB), reads=["OB"], writes=["OSB"])
                    add("pool", lambda e: e.tensor_tensor(out=SQB, in0=OSB, in1=OSB, op=ALU.mult), reads=["OSB"], writes=["SQB"])
                    pending.setdefault(n + 1, []).append(
                        lambda: add("pe", lambda e: e.matmul(SSB, lhsT=BDONES, rhs=SQB, start=True, stop=True), reads=["SQB", "BDONES"], writes=["SSB"]))
                    def act_part():
                        add("act", lambda e: e.activation(out=R, in_=SSB, func=AF.Ln, scale=1.0 / HD, bias=EPS), reads=["SSB"], writes=["R"])
                        add("act", lambda e: e.activation(out=R, in_=R, func=AF.Exp, scale=-0.5), reads=["R"], writes=["R"])
                    pending.setdefault(n + 2, []).append(act_part)
                    pending.setdefault(n + 3, []).append(
                        lambda: add("dve", lambda e: e.scalar_tensor_tensor(out=MIXT[:, p, t0:t0 + 512], in0=OSB, scalar=GAINS[:, 3:4], in1=R,
                                                                            op0=ALU.mult, op1=ALU.mult), reads=["OSB", "R"] + GAINS_R, writes=["MIXT"]))

            pending = {}

            for n0 in range(3):
                st_z(n0)
            st_exp1(0)
            for n in range(-1, ns):
                if n + 1 < ns:
                    st_ln(n + 1)
                    st_pe2(n + 1)
                if n >= 0:
                    st_exp3(n)
                    for pc in pending.pop(n, []):
                        pc()
                    st_av(n)
                    if n + 3 < ns:
                        st_z(n + 3)
                if n + 2 < ns:
                    st_exp1(n + 2)
                if n + 1 < ns:
                    st_cast(n + 1)
            for kk in sorted(pending):
                for pc in pending[kk]:
                    pc()

        def phase_c(b):
            HO = [bank(0, 2), bank(2, 2)]
            TPH = bank_bf(4).rearrange("p (c t) -> p c t", c=8)
            FB = [bank(4), bank(5)]
            O2 = [bank(6)[:, 0:256], bank(7)[:, 0:256]]
            s1v = scr1.rearrange("(c p) f -> p c f", p=128)
            s2v = scr2.rearrange("(c p) m -> p c m", p=128)

            def load_xr(T, sub):
                r = sub % 2
                tok0 = T * 512 + sub * 128
                add("sp", lambda e: e.dma_start(out=XR[r], in_=x[b, tok0:tok0 + 128, :]), writes=[("XR", r)], dma="XR%d" % r)

            def load_w1(fg):
                r = fg % 3
                add("act", lambda e: e.dma_start(out=W1[r], in_=s1v[:, :, fg * 512:(fg + 1) * 512]), reads=SCR1_R, writes=[("W1", r)], dma="W1%d" % r)

            def load_w2(q):
                r = q % 2
                add("sp", lambda e: e.dma_start(out=W2[r], in_=s2v[:, :, q * 256:(q + 1) * 256]), reads=SCR2_R, writes=[("W2", r)], dma="W2%d" % r)

            def c1_mm(T, sub):
                hb = sub % 2
                tok = slice(T * 512 + sub * 128, T * 512 + (sub + 1) * 128)
                def mm(e):
                    ins = None
                    for half in range(2):
                        for c in range(8):
                            ins = e.matmul(HO[hb][:, half * 512:(half + 1) * 512], lhsT=MIXT[:, c, tok], rhs=WOUT[:, c, half * 512:(half + 1) * 512],
                                           start=(c == 0), stop=(c == 7))
                    return ins
                add("pe", mm, reads=["MIXT"] + WOUT_R, writes=[("HO", hb)])

            def c1_mid(T, sub):
                hb = sub % 2
                r = sub % 2
                tb = T % 2
                add("dve", lambda e: e.tensor_tensor(out=H1[tb][:, sub, :], in0=HO[hb], in1=XR[r], op=ALU.add),
                    reads=[("HO", hb), ("XR", r)], writes=[("H1", tb, sub)])
                if sub + 2 < 4:
                    load_xr(T, sub + 2)
                add("act", lambda e: e.activation(out=H1B, in_=H1[tb][:, sub, :], func=AF.Copy), reads=[("H1", tb, sub)], writes=["H1B"])
                add("act", lambda e: e.activation(out=JUNKC, in_=H1[tb][:, sub, :], func=AF.Square, accum_out=SS2[:, sub:sub + 1]),
                    reads=[("H1", tb, sub)], writes=[("SS2", sub)])
                add("act", lambda e: e.activation(out=RT2[:, sub:sub + 1], in_=SS2[:, sub:sub + 1], func=AF.Sqrt, scale=1.0 / D, bias=EPS),
                    reads=[("SS2", sub)], writes=[("RT2", sub)])
                add("dve", lambda e: e.reciprocal(out=R2[:, sub:sub + 1], in_=RT2[:, sub:sub + 1]), reads=[("RT2", sub)], writes=[("R2", sub)])
                add("dve", lambda e: e.tensor_tensor(out=R2SQ[tb][:, sub:sub + 1], in0=R2[:, sub:sub + 1], in1=R2[:, sub:sub + 1], op=ALU.mult),
                    reads=[("R2", sub)], writes=[("R2SQ", tb, sub)])

            def c1_tr(T, sub):
                tb = T % 2
                def tr(e):
                    ins = None
                    for c in range(8):
                        ins = e.transpose(out=TPH[:, c, :], in_=H1B[:, c * 128:(c + 1) * 128], identity=IDENT)
                    return ins
                add("pe", tr, reads=["H1B", "IDENT"], writes=[("FB", 0)])
                add("dve", lambda e: e.tensor_tensor(out=H1T[tb][:, :, sub * 128:(sub + 1) * 128], in0=TPH, in1=bc_last(G2T, 128), op=ALU.mult),
                    reads=[("FB", 0), "G2T"], writes=[("H1T", tb)])

            def c1_pieces(T):
                return {
                    0: [lambda: (load_xr(T, 0), load_xr(T, 1))],
                    1: [lambda: c1_mm(T, 0)],
                    2: [lambda: c1_mm(T, 1)],
                    3: [lambda: c1_mid(T, 0)],
                    4: [lambda: c1_mm(T, 2)],
                    5: [lambda: c1_tr(T, 0), lambda: c1_mid(T, 1)],
                    6: [lambda: c1_mm(T, 3)],
                    7: [lambda: c1_tr(T, 1), lambda: c1_mid(T, 2)],
                    9: [lambda: c1_tr(T, 2), lambda: c1_mid(T, 3)],
                    11: [lambda: c1_tr(T, 3)],
                }

            def c2(T):
                tb = T % 2
                for fg in range(8):
                    r = fg % 3
                    for fj in range(4):
                        fc = fg * 4 + fj
                        fb = fc % 2
                        def mm(e, r=r, fj=fj, fb=fb):
                            ins = None
                            for c in range(8):
                                ins = e.matmul(FB[fb], lhsT=W1[r][:, c, fj * 128:(fj + 1) * 128], rhs=H1T[tb][:, c, :], start=(c == 0), stop=(c == 7))
                            return ins
                        add("pe", mm, reads=[("W1", r), ("H1T", tb)], writes=[("FB", fb)])
                        add("act", lambda e, fb=fb: e.activation(out=RL[fb], in_=FB[fb], func=AF.Relu), reads=[("FB", fb)], writes=[("RL", fb)])
                        add("dve", lambda e, fb=fb, fc=fc: e.tensor_tensor(out=AFF[:, fc, :], in0=RL[fb], in1=RL[fb], op=ALU.mult),
                            reads=[("RL", fb)], writes=["AFF"])
                    if fg + 3 < 8:
                        load_w1(fg + 3)

            def c3(T, pieces):
                tb = T % 2
                k = 0
                for q in range(4):
                    r = q % 2
                    for sub in range(4):
                        ob = (q * 4 + sub) % 2
                        def mm(e, r=r, sub=sub, ob=ob):
                            ins = None
                            for fc in range(32):
                                ins = e.matmul(O2[ob], lhsT=AFF[:, fc, sub * 128:(sub + 1) * 128], rhs=W2[r][:, fc, :], start=(fc == 0), stop=(fc == 31))
                            return ins
                        add("pe", mm, reads=["AFF", ("W2", r)], writes=[("O2", ob)])
                        qs = slice(q * 256, (q + 1) * 256)
                        add("dve", lambda e, sub=sub, ob=ob, qs=qs: e.scalar_tensor_tensor(out=H1[tb][:, sub, qs], in0=O2[ob], scalar=R2SQ[tb][:, sub:sub + 1],
                                                                                         in1=H1[tb][:, sub, qs], op0=ALU.mult, op1=ALU.add),
                            reads=[("O2", ob), ("R2SQ", tb, sub), ("H1", tb, sub)], writes=[("H1", tb, sub), ("O2", ob)])
                        for pc in pieces.get(k, []):
                            pc()
                        k += 1
                    if q + 2 < 4:
                        load_w2(q + 2)
                    elif T + 1 < 4:
                        load_w2(q - 2)
                for sub in range(4):
                    tok0 = T * 512 + sub * 128
                    add("sp", lambda e, sub=sub, tok0=tok0: e.dma_start(out=out[b, tok0:tok0 + 128, :], in_=H1[tb][:, sub, :]),
                        reads=[("H1", tb, sub)], writes=["OUT"], dma="out%d" % sub)

            load_w1(0)
            load_w1(1)
            load_w1(2)
            p0 = c1_pieces(0)
            for kk in sorted(p0):
                for pc in p0[kk]:
                    pc()
            load_w2(0)
            load_w2(1)
            for T in range(4):
                c2(T)
                if T == 3 and b + 1 < NB:
                    win_load(b + 1, alias=True)
                if T + 1 < 4:
                    load_w1(0)
                    load_w1(1)
                    load_w1(2)
                c3(T, c1_pieces(T + 1) if T + 1 < 4 else {})

        setup()
        for b in range(NB):
            phase_a(b)
            SC.barrier()
            phase_b(b)
            SC.barrier()
            phase_c(b)
            SC.barrier()

        eng_sems = {e: es.enter_context(nc.semaphore("s_" + e)) for e in ENGS}
        dma_sems = {s: es.enter_context(nc.semaphore("d_" + s)) for s in SC.dma_slots()}
        SC.assign_tokens(eng_sems, dma_sems)

        block = es.enter_context(nc.Block())

        @block.tensor
        def _(e):
            SC.emit("pe", e)

        @block.scalar
        def _(e):
            SC.emit("act", e)

        @block.vector
        def _(e):
            SC.emit("dve", e)

        @block.gpsimd
        def _(e):
            SC.emit("pool", e)

        @block.sync
        def _(e):
            SC.emit("sp", e)

    return nc


_NC = None


def kernel(x, norm1_g, w_in, q_norm_g, k_norm_g, sg_v_norm_g, sg_w, sg_b,
           sb_out_norm_g, sg_out_norm_g, w_out, norm2_g, w_ff1, w_ff2):
    global _NC
    if _NC is None:
        _NC = build_nc()
    f = lambda a: np.ascontiguousarray(np.asarray(a, dtype=np.float32))
    x = f(x)
    shared = {
        "norm1_g": f(norm1_g), "w_in": f(w_in), "q_norm_g": f(q_norm_g), "k_norm_g": f(k_norm_g),
        "sg_v_norm_g": f(sg_v_norm_g), "sg_w": f(sg_w), "sg_b": f(sg_b), "sb_out_norm_g": f(sb_out_norm_g),
        "sg_out_norm_g": f(sg_out_norm_g), "w_out": f(w_out), "norm2_g": f(norm2_g), "w_ff1": f(w_ff1), "w_ff2": f(w_ff2),
    }
    in_maps = []
    for c in range(NCORES):
        m = dict(shared)
        m["x"] = np.ascontiguousarray(x[c * NB:(c + 1) * NB])
        in_maps.append(m)
    res = run_bass_kernel_spmd(_NC, in_maps, core_ids=list(range(NCORES)))
    outs = [np.asarray(res.results[c]["out"], dtype=np.float32) for c in range(NCORES)]
    return np.concatenate(outs, axis=0)
```

```python
import numpy as np
from contextlib import ExitStack
import concourse.bass as bass
import concourse.mybir as mybir
from concourse.bass_utils import run_bass_kernel_spmd

F32 = mybir.dt.float32
BF16 = mybir.dt.bfloat16
U8 = mybir.dt.uint8
AF = mybir.ActivationFunctionType
ALU = mybir.AluOpType
AX = mybir.AxisListType

NCORES = 8
NB = 2
S = 2048
D = 1024
HD = 64
DFF = 4096
INW = 2560
EPS = 1e-6
NT = S // 128

ENGS = ("pe", "act", "dve", "pool", "sp")


class Op:
    __slots__ = ("eng", "fn", "deps", "needs_inc", "token", "dma", "idx")

    def __init__(self, eng, fn, dma):
        self.eng = eng
        self.fn = fn
        self.deps = []
        self.needs_inc = False
        self.token = None
        self.dma = dma


class Sched:
    def __init__(self):
        self.ops = {e: [] for e in ENGS}
        self.res = {}
        self.last_real = {e: None for e in ENGS}
        self.last_compute = {e: None for e in ENGS}
        self.dmas_since_barrier = []
        self.group_slots = set()

    def add(self, eng, fn, reads=(), writes=(), dma=None, group=False):
        op = Op(eng, fn, dma)
        deps = []
        for r in reads:
            st = self.res.get(r)
            if st is not None and st[0] is not None:
                deps.append(st[0])
        for w in writes:
            st = self.res.get(w)
            if st is not None:
                if st[0] is not None:
                    deps.append(st[0])
                deps.extend(st[1])
        seen = set()
        for d in deps:
            if id(d) in seen:
                continue
            seen.add(id(d))
            if d.eng == "pe" and eng == "pe" and d.dma is None:
                continue
            op.deps.append(d)
            d.needs_inc = True
        for r in reads:
            st = self.res.setdefault(r, [None, []])
            st[1].append(op)
        for w in writes:
            self.res[w] = [op, []]
        op.idx = len(self.ops[eng])
        self.ops[eng].append(op)
        if fn is not None:
            self.last_real[eng] = op
            if dma is None:
                self.last_compute[eng] = op
        if dma is not None:
            self.dmas_since_barrier.append(op)
            if group:
                self.group_slots.add(dma)
        return op

    def barrier(self):
        lasts = dict(self.last_compute)
        dmas = list(self.dmas_since_barrier)
        self.dmas_since_barrier = []
        for e in ENGS:
            op = Op(e, None, None)
            op.idx = len(self.ops[e])
            for e2 in ENGS:
                l = lasts[e2]
                if l is not None and e2 != e and l.dma is None:
                    op.deps.append(l)
                    l.needs_inc = True
            for d in dmas:
                op.deps.append(d)
            self.ops[e].append(op)

    def assign_tokens(self, eng_sems, dma_sems):
        slot_cnt = {}
        slot_ops = {}
        for e in ENGS:
            cnt = 0
            for op in self.ops[e]:
                if op.dma is not None:
                    c = slot_cnt.get(op.dma, 0) + 16
                    slot_cnt[op.dma] = c
                    op.token = (dma_sems[op.dma], c)
                    slot_ops.setdefault(op.dma, []).append(op)
                elif op.needs_inc:
                    cnt += 1
                    op.token = (eng_sems[e], cnt)
        for slot in self.group_slots:
            for op in slot_ops[slot]:
                op.token = (dma_sems[slot], slot_cnt[slot])

    def dma_slots(self):
        s = []
        for e in ENGS:
            for op in self.ops[e]:
                if op.dma is not None and op.dma not in s:
                    s.append(op.dma)
        return s

    def emit(self, eng, e):
        waited = {}
        for op in self.ops[eng]:
            need = {}
            for d in op.deps:
                sem, val = d.token
                k = id(sem)
                if k not in need or need[k][1] < val:
                    need[k] = (sem, val)
            for k, (sem, val) in need.items():
                if waited.get(k, 0) >= val:
                    continue
                e.wait_ge(sem, val)
                waited[k] = val
            if op.fn is not None:
                ins = op.fn(e)
                if op.dma is not None:
                    ins.then_inc(op.token[0], 16)
                elif op.needs_inc:
                    ins.then_inc(op.token[0], 1)


class Arena:
    def __init__(self, ap, limit):
        self.ap = ap
        self.limit = limit

    def view(self, off, nbytes, dt, pattern=None, **kw):
        assert off % 4 == 0 and off + nbytes <= self.limit, (off, nbytes, self.limit)
        v = self.ap[:, off:off + nbytes].bitcast(dt)
        if pattern is not None:
            v = v.rearrange(pattern, **kw)
        return v


def bc_last(ap2, n):
    return ap2.unsqueeze(2).to_broadcast([ap2.shape[0], ap2.shape[1], n])


def bc_mid(ap2, n):
    return ap2.unsqueeze(1).to_broadcast([ap2.shape[0], n, ap2.shape[1]])


def build_nc():
    nc = bass.Bass("TRN2", target_bir_lowering=False)
    dt = nc.dram_tensor
    x = dt("x", [NB, S, D], F32, kind="ExternalInput").ap()
    norm1_g = dt("norm1_g", [D], F32, kind="ExternalInput").ap()
    w_in = dt("w_in", [D, INW], F32, kind="ExternalInput").ap()
    q_norm_g = dt("q_norm_g", [HD], F32, kind="ExternalInput").ap()
    k_norm_g = dt("k_norm_g", [HD], F32, kind="ExternalInput").ap()
    sg_v_norm_g = dt("sg_v_norm_g", [HD], F32, kind="ExternalInput").ap()
    sg_w = dt("sg_w", [8, 128, 128], F32, kind="ExternalInput").ap()
    sg_b = dt("sg_b", [8, 128], F32, kind="ExternalInput").ap()
    sb_out_norm_g = dt("sb_out_norm_g", [HD], F32, kind="ExternalInput").ap()
    sg_out_norm_g = dt("sg_out_norm_g", [HD], F32, kind="ExternalInput").ap()
    w_out = dt("w_out", [D, D], F32, kind="ExternalInput").ap()
    norm2_g = dt("norm2_g", [D], F32, kind="ExternalInput").ap()
    w_ff1 = dt("w_ff1", [D, DFF], F32, kind="ExternalInput").ap()
    w_ff2 = dt("w_ff2", [DFF, D], F32, kind="ExternalInput").ap()
    out = dt("out", [NB, S, D], F32, kind="ExternalOutput").ap()
    scr1 = dt("scr1", [D, DFF], BF16).ap()
    scr2 = dt("scr2", [DFF, D], BF16).ap()

    SC = Sched()
    es = ExitStack()
    with es:
        ARENA_BYTES = 207 * 1024
        arena_t = es.enter_context(nc.sbuf_tensor("arena", [128, ARENA_BYTES], U8))
        ps_t = es.enter_context(nc.psum_tensor("ps", [128, 4096], F32))
        AR = Arena(arena_t, ARENA_BYTES)

        def bank(k, n=1):
            return ps_t[:, k * 512:(k + n) * 512]

        def bank_bf(k):
            return ps_t[:, k * 512:(k + 1) * 512].bitcast(BF16)

        off = [0]

        def take(nbytes):
            o = off[0]
            off[0] += (nbytes + 31) // 32 * 32
            return o

        KB = 1024
        o_ident32 = take(512)
        o_ident = take(256); o_negtri = take(256); o_negones = take(256); o_bdones = take(256); o_mask = take(256)
        o_wt = take(2 * KB); o_sgb = take(32); o_gains = take(64)
        o_g1t = take(32); o_g2t = take(32); o_gvbc = take(256)
        o_small = take(1 * KB)
        o_wout = take(16 * KB)
        o_mixt = take(32 * KB)
        base_state = off[0]
        o_qt = take(16 * KB); o_kt = take(16 * KB); o_vpad = take(32 * KB)
        base_ab = off[0]
        o_win = take(40 * KB)
        o_x = [take(4 * KB), take(4 * KB)]
        o_xb = take(2 * KB)
        o_xt = [take(2 * KB), take(2 * KB)]
        o_qf = take(2 * KB); o_kf = take(2 * KB); o_vsf = take(2 * KB); o_uf = [take(2 * KB), take(2 * KB), take(2 * KB)]
        o_sq = take(6 * KB)
        o_stgf = o_sq; o_stgb = o_sq + 4 * KB
        o_n3 = [take(3 * KB), take(3 * KB)]; o_on = [take(1 * KB), take(1 * KB)]
        o_junk = take(2 * KB)
        o_t1 = take(2 * KB)
        end_a = off[0]
        off[0] = base_ab
        o_e = [take(4 * KB), take(4 * KB)]
        o_l = [take(2 * KB), take(2 * KB), take(2 * KB)]
        o_ls = take(4 * KB)
        o_lsb = [take(2 * KB), take(2 * KB)]
        o_a = [take(2 * KB), take(2 * KB)]
        o_qp = [take(2 * KB), take(2 * KB)]
        o_osb = take(2 * KB); o_sqb = take(1 * KB); o_r = take(2 * KB)
        end_b = off[0]
        off[0] = base_state
        o_af = take(32 * KB); o_h1 = [take(16 * KB), take(16 * KB)]
        assert off[0] == o_win, (off[0], o_win)
        o_w1 = [take(8 * KB), take(8 * KB), take(8 * KB)]
        o_h1t = [take(8 * KB), take(8 * KB)]
        assert off[0] == o_win + 40 * KB
        o_xr = [take(4 * KB), take(4 * KB)]
        o_h1b = take(2 * KB)
        o_junkc = take(2 * KB)
        o_rl = [take(1 * KB), take(1 * KB)]
        o_w2 = [take(16 * KB), take(16 * KB)]
        end_c = off[0]
        assert max(end_a, end_b, end_c) <= ARENA_BYTES, (end_a, end_b, end_c)

        V = AR.view
        IDENT32 = V(o_ident32, 512, F32)
        IDENT = V(o_ident, 256, BF16); NEGTRI = V(o_negtri, 256, BF16); NEGONES = V(o_negones, 256, BF16)
        BDONES = V(o_bdones, 256, BF16); MASK = V(o_mask, 256, BF16)
        WT = V(o_wt, 2 * KB, BF16, "p (g t) -> p g t", g=8)
        SGB = V(o_sgb, 32, F32)
        GAINS = V(o_gains, 64, F32)
        G1T = V(o_g1t, 32, F32); G2T = V(o_g2t, 32, F32); GVBC = V(o_gvbc, 256, F32)
        SMALL = V(o_small, 1 * KB, F32)
        STGF = V(o_stgf, 4 * KB, F32, "p (g s) -> p g s", g=8)
        STGB = V(o_stgb, 2 * KB, BF16, "p (g s) -> p g s", g=8)
        WOUT = V(o_wout, 16 * KB, BF16, "p (c f) -> p c f", c=8)
        MIXT = V(o_mixt, 32 * KB, BF16, "p (c t) -> p c t", c=8)
        QT = V(o_qt, 16 * KB, BF16, "p (c t) -> p c t", c=4)
        KT = V(o_kt, 16 * KB, BF16, "p (c t) -> p c t", c=4)
        VPAD = V(o_vpad, 32 * KB, BF16, "p (i c h d) -> p i c h d", i=16, c=4, h=2)
        VPAD_FLAT = V(o_vpad, 32 * KB, BF16)
        WIN = V(o_win, 40 * KB, BF16, "p (c f) -> p c f", c=8)
        X = [V(o, 4 * KB, F32) for o in o_x]
        SQT = V(o_xb, 2 * KB, F32)
        XT = [V(o, 2 * KB, BF16, "p (c t) -> p c t", c=8) for o in o_xt]
        XTF = [V(o, 2 * KB, BF16) for o in o_xt]
        QF = V(o_qf, 2 * KB, F32); KF = V(o_kf, 2 * KB, F32); VSF = V(o_vsf, 2 * KB, F32); UF = [V(o, 2 * KB, F32) for o in o_uf]
        SQ3 = V(o_sq, 6 * KB, F32)
        SQ = V(o_sq, 2 * KB, F32)
        F3 = V(o_qf, 6 * KB, F32)
        N3 = [V(o, 3 * KB, BF16) for o in o_n3]
        JUNK = V(o_junk, 2 * KB, BF16)
        ON = [V(o, 1 * KB, BF16) for o in o_on]
        T1 = V(o_t1, 2 * KB, F32)
        E = [V(o, 4 * KB, F32, "p (h t) -> p h t", h=2) for o in o_e]
        L = [V(o, 2 * KB, BF16, "p (h t) -> p h t", h=2) for o in o_l]
        LS = V(o_ls, 4 * KB, F32, "p (h t) -> p h t", h=2)
        LSB = [V(o, 2 * KB, BF16, "p (h t) -> p h t", h=2) for o in o_lsb]
        A = [V(o, 2 * KB, BF16, "p (h t) -> p h t", h=2) for o in o_a]
        QP = [V(o, 2 * KB, BF16, "p (h t) -> p h t", h=2) for o in o_qp]
        OSB = V(o_osb, 2 * KB, F32); SQB = V(o_sqb, 1 * KB, BF16); R = V(o_r, 2 * KB, F32)
        AFF = V(o_af, 32 * KB, BF16, "p (c t) -> p c t", c=32)
        H1T = [V(o, 8 * KB, BF16, "p (c t) -> p c t", c=8) for o in o_h1t]
        H1 = [V(o, 16 * KB, F32, "p (s f) -> p s f", s=4) for o in o_h1]
        XR = [V(o, 4 * KB, F32) for o in o_xr]
        H1B = V(o_h1b, 2 * KB, BF16)
        JUNKC = V(o_junkc, 2 * KB, BF16)
        RL = [V(o, 1 * KB, BF16) for o in o_rl]
        W1 = [V(o, 8 * KB, BF16, "p (c f) -> p c f", c=8) for o in o_w1]
        W2 = [V(o, 16 * KB, BF16, "p (c f) -> p c f", c=32) for o in o_w2]

        sm = [0]

        def small(n):
            c = sm[0]
            sm[0] += n
            assert sm[0] <= 256
            return SMALL[:, c:c + n]

        SS1 = [small(1), small(1)]
        RT1 = [small(1), small(1)]
        RSTD1 = [small(1), small(1)]
        SSQ8 = small(8); RT8 = small(8); R8 = small(8)
        SSQ24 = small(24); RT24 = small(24); R24 = small(24)
        SS2 = small(4); RT2 = small(4); R2 = small(4); R2SQ = [small(4), small(4)]

        add = SC.add
        GAINS_R = [("GAINS", col, hh) for col in (0, 1, 3, 4) for hh in range(2)]
        WIN_R = [("WIN", c) for c in range(8)]
        WOUT_R = [("WOUT", c) for c in range(8)]
        SCR1_R = [("scr1", i) for i in range(4)]
        SCR2_R = [("scr2", i) for i in range(4)]

        def scr_casts():
            for i in range(4):
                add("pool", lambda e, i=i: e.dma_start(out=scr1[i * 256:(i + 1) * 256, :], in_=w_ff1[i * 256:(i + 1) * 256, :]),
                    writes=[("scr1", i)], dma="scr", group=True)
            for i in range(4):
                add("pool", lambda e, i=i: e.dma_start(out=scr2[i * 1024:(i + 1) * 1024, :], in_=w_ff2[i * 1024:(i + 1) * 1024, :]),
                    writes=[("scr2", i)], dma="scr", group=True)

        def wout_load():
            for c in range(8):
                add("pool", lambda e, c=c: e.dma_start(out=WOUT[:, c, :], in_=w_out[c * 128:(c + 1) * 128, :]),
                    writes=[("WOUT", c)], dma="wout", group=True)

        def setup():
            add("sp", lambda e: e.dma_start(out=G1T, in_=norm1_g.rearrange("(c p) -> p c", p=128), allow_slow_non_contiguous=True),
                writes=["G1T"], dma="g1t")
            add("act", lambda e: e.dma_start(out=STGF, in_=sg_w.rearrange("g t s -> t g s")), writes=["SQ"], dma="const", group=True)
            add("act", lambda e: e.dma_start(out=GVBC, in_=sg_v_norm_g.partition_broadcast(128)), writes=["GVBC"], dma="const", group=True)
            add("act", lambda e: e.dma_start(out=SGB, in_=sg_b.rearrange("g t -> t g"), allow_slow_non_contiguous=True),
                writes=["SGB"], dma="const", group=True)
            add("act", lambda e: e.dma_start(out=G2T, in_=norm2_g.rearrange("(c p) -> p c", p=128), allow_slow_non_contiguous=True),
                writes=["G2T"], dma="const", group=True)
            for col, gsrc in ((0, q_norm_g), (1, k_norm_g), (3, sb_out_norm_g), (4, sg_out_norm_g)):
                for hh in range(2):
                    add("act", lambda e, col=col, gsrc=gsrc, hh=hh: e.dma_start(
                        out=GAINS[hh * 64:(hh + 1) * 64, col:col + 1], in_=gsrc.rearrange("(d o) -> d o", o=1)),
                        writes=[("GAINS", col, hh)], dma="const", group=True)
            add("pool", lambda e: e.memset(IDENT32, 1.0), writes=["IDENT32"])
            add("pool", lambda e: e.affine_select(out=IDENT32, in_=IDENT32, pattern=[[1, 128]], compare_op=ALU.is_equal, fill=0.0, base=0, channel_multiplier=-1),
                reads=["IDENT32"], writes=["IDENT32"])
            add("pool", lambda e: e.memset(IDENT, 1.0), writes=["IDENT"])
            add("pool", lambda e: e.affine_select(out=IDENT, in_=IDENT, pattern=[[1, 128]], compare_op=ALU.is_equal, fill=0.0, base=0, channel_multiplier=-1),
                reads=["IDENT"], writes=["IDENT"])
            add("pool", lambda e: e.memset(NEGTRI, -1.0), writes=["NEGTRI"])
            add("pool", lambda e: e.affine_select(out=NEGTRI, in_=NEGTRI, pattern=[[-1, 128]], compare_op=ALU.is_ge, fill=0.0, base=0, channel_multiplier=1),
                reads=["NEGTRI"], writes=["NEGTRI"])
            add("pool", lambda e: e.memset(NEGONES, -1.0), writes=["NEGONES"])
            add("pool", lambda e: e.memset(BDONES, 0.0), writes=["BDONES"])
            add("pool", lambda e: e.memset(BDONES[0:64, 0:64], 1.0), reads=["BDONES"], writes=["BDONES"])
            add("pool", lambda e: e.memset(BDONES[64:128, 64:128], 1.0), reads=["BDONES"], writes=["BDONES"])
            add("pool", lambda e: e.memset(MASK, 1.0), writes=["MASK"])
            add("pool", lambda e: e.affine_select(out=MASK, in_=MASK, pattern=[[1, 128]], compare_op=ALU.is_gt, fill=0.0, base=0, channel_multiplier=-1),
                reads=["MASK"], writes=["MASK"])
            add("dve", lambda e: e.tensor_tensor(out=GAINS[:, 2:3], in0=GAINS[:, 0:1], in1=GAINS[:, 1:2], op=ALU.mult),
                reads=GAINS_R, writes=["GQK0"])
            add("dve", lambda e: e.tensor_scalar(out=GAINS[:, 2:3], in0=GAINS[:, 2:3], scalar1=0.125, scalar2=None, op0=ALU.mult),
                reads=["GQK0"], writes=["GQK"])
            add("pool", lambda e: e.tensor_copy(out=STGB, in_=STGF), reads=["SQ"], writes=["SQ"])
            tp = bank_bf(7).rearrange("p (g t) -> p g t", g=8)
            def tr_w(e):
                ins = None
                for g in range(8):
                    ins = e.transpose(out=tp[:, g, :], in_=STGB[:, g, :], identity=IDENT)
                return ins
            add("pe", tr_w, reads=["SQ", "IDENT"], writes=["B7"])
            add("dve", lambda e: e.tensor_copy(out=WT, in_=tp), reads=["B7"], writes=["WT0"])
            add("pool", lambda e: e.affine_select(out=WT, in_=WT, pattern=[[0, 8], [1, 128]], compare_op=ALU.is_ge, fill=0.0,
                                                  base=0, channel_multiplier=-1), reads=["WT0"], writes=["WT"])

        def win_load(b, alias=False):
            for c in range(8):
                wr = [("WIN", c)]
                if alias and c == 0:
                    wr += [("W1", 0), ("W1", 1), ("W1", 2), ("H1T", 0), ("H1T", 1)]
                add("pool", lambda e, c=c: e.dma_start(out=WIN[:, c, :], in_=w_in[c * 128:(c + 1) * 128, :]),
                    writes=wr, dma="win%d" % b, group=True)

        def phase_a(b):
            if b == 0:
                wout_load()
            P = [bank(j) for j in range(5)]
            TPX = bank_bf(5).rearrange("p (c t) -> p c t", c=8)
            TPXA = bank(5).rearrange("p (c t) -> p c t", c=4)
            TPXB = bank(7).rearrange("p (c t) -> p c t", c=4)
            Y = bank(2)
            TQK = bank_bf(6).rearrange("p (c t) -> p c t", c=8)
            TS = bank_bf(7)[:, 0:512].rearrange("p (c t) -> p c t", c=4)
            g8 = lambda ap: ap.rearrange("p (g d) -> p g d", d=64)

            def load_x(i):
                r = i % 2
                add("sp", lambda e: e.dma_start(out=X[r], in_=x[b, i * 128:(i + 1) * 128, :]), writes=[("X", r)], dma="X%d" % r)

            def front0a(i):
                r = i % 2
                add("act", lambda e: e.activation(out=JUNK, in_=X[r], func=AF.Square, accum_out=SS1[r]),
                    reads=[("X", r)], writes=[("SS1", r)])

            def front0b(i):
                r = i % 2
                add("act", lambda e: e.activation(out=RT1[r], in_=SS1[r], func=AF.Sqrt, scale=1.0 / D, bias=EPS),
                    reads=[("SS1", r)], writes=[("RT1", r)])
                add("dve", lambda e: e.reciprocal(out=RSTD1[r], in_=RT1[r]), reads=[("RT1", r)], writes=[("RSTD1", r)])

            def front0(i):
                front0a(i)
                front0b(i)

            def tp(i):
                r = i % 2
                def tra(e):
                    ins = None
                    for c in range(4):
                        ins = e.transpose(out=TPXA[:, c, :], in_=X[r][:, c * 128:(c + 1) * 128], identity=IDENT32)
                    return ins
                def trb(e):
                    ins = None
                    for c in range(4):
                        ins = e.transpose(out=TPXB[:, c, :], in_=X[r][:, (4 + c) * 128:(5 + c) * 128], identity=IDENT32)
                    return ins
                add("pe", tra, reads=[("X", r), "IDENT32"], writes=["B5"])
                add("pe", trb, reads=[("X", r), "IDENT32"], writes=["B7"])
                add("dve", lambda e: e.tensor_tensor(out=XT[r][:, 0:4, :], in0=TPXA, in1=bc_last(G1T[:, 0:4], 128), op=ALU.mult),
                    reads=["B5", "G1T"], writes=[("XT", r, 0)])
                add("dve", lambda e: e.tensor_tensor(out=XT[r][:, 4:8, :], in0=TPXB, in1=bc_last(G1T[:, 4:8], 128), op=ALU.mult),
                    reads=["B7", "G1T"], writes=[("XT", r, 1)])

            def proj(i):
                r = i % 2
                for j in range(5):
                    def mm(e, j=j):
                        ins = None
                        for c in range(8):
                            ins = e.matmul(P[j], lhsT=XT[r][:, c, :], rhs=WIN[:, c, j * 512:(j + 1) * 512], start=(c == 0), stop=(c == 7))
                        return ins
                    add("pe", mm, reads=[("XT", r, 0), ("XT", r, 1)] + WIN_R, writes=[("P", j)])

            def front2(i):
                r = i % 2
                sc = RSTD1[r]
                p2 = P[2].rearrange("p (c h d) -> p c h d", c=4, h=2)
                for hh in range(2):
                    add("pool", lambda e, hh=hh: e.memset(VPAD[:, i, :, hh, (1 - hh) * 64:(2 - hh) * 64], 0.0), writes=[("VPADZ", hh)])
                for hh in range(2):
                    add("act", lambda e, hh=hh: e.activation(out=VPAD[:, i, :, hh, hh * 64:(hh + 1) * 64], in_=p2[:, :, hh, :], func=AF.Copy, scale=sc),
                        reads=[("P", 2), ("RSTD1", r), "VPAD"], writes=[("VPADW", hh)])
                add("act", lambda e: e.activation(out=QF, in_=P[0], func=AF.Copy, scale=sc), reads=[("P", 0), ("RSTD1", r)], writes=["QF"])
                add("act", lambda e: e.activation(out=KF, in_=P[1], func=AF.Copy, scale=sc), reads=[("P", 1), ("RSTD1", r)], writes=["KF"])
                add("act", lambda e: e.activation(out=VSF, in_=P[4], func=AF.Copy, scale=sc), reads=[("P", 4), ("RSTD1", r)], writes=["VSF"])
                u3 = i % 3
                add("act", lambda e: e.activation(out=UF[u3], in_=P[3], func=AF.Copy, scale=sc), reads=[("P", 3), ("RSTD1", r)], writes=[("UF", u3)])

            def early_a1(i):
                add("act", lambda e: e.activation(out=SQ3, in_=F3, func=AF.Square), reads=["QF", "KF", "VSF"], writes=["SQ"])
                add("dve", lambda e: e.tensor_reduce(out=SSQ24, in_=g8(SQ3), axis=AX.X, op=ALU.add), reads=["SQ"], writes=["SSQ24"])

            def early_a2(i):
                add("act", lambda e: e.activation(out=RT24, in_=SSQ24, func=AF.Sqrt, scale=1.0 / HD, bias=EPS), reads=["SSQ24"], writes=["RT24"])
                add("dve", lambda e: e.reciprocal(out=R24, in_=RT24), reads=["RT24"], writes=["R24"])
                nb = i % 2
                add("dve", lambda e: e.tensor_tensor(out=g8(N3[nb]), in0=g8(F3), in1=bc_last(R24, 64), op=ALU.mult),
                    reads=["QF", "KF", "VSF", "R24"], writes=[("N3", nb)])

            def early_b(i):
                tok = slice(i * 128, (i + 1) * 128)
                nb = i % 2
                def trqk(e):
                    ins = None
                    for p in range(8):
                        ins = e.transpose(out=TQK[:, p, :], in_=N3[nb][:, p * 128:(p + 1) * 128], identity=IDENT)
                    return ins
                add("pe", trqk, reads=[("N3", nb), "IDENT"], writes=["B6"])
                add("dve", lambda e: e.tensor_copy(out=QT[:, :, tok], in_=TQK[:, 0:4, :]), reads=["B6"], writes=["QT", "B6"])
                add("act", lambda e: e.activation(out=KT[:, :, tok], in_=TQK[:, 4:8, :], func=AF.Copy, scale=GAINS[:, 2:3]),
                    reads=["B6", "GQK"], writes=["KT", "B6"])

            def late1a(i):
                nb = i % 2
                u3 = i % 3
                def sg(e):
                    ins = None
                    for g in range(8):
                        ins = e.matmul(Y[:, g * 64:(g + 1) * 64], lhsT=WT[:, g, :], rhs=N3[nb][:, 1024 + g * 64:1024 + (g + 1) * 64], start=True, stop=True)
                    return ins
                add("pe", sg, reads=[("N3", nb), "WT"], writes=[("P", 2)])
                add("dve", lambda e: e.tensor_tensor(out=g8(T1), in0=g8(Y), in1=bc_mid(GVBC, 8), op=ALU.mult), reads=[("P", 2), "GVBC"], writes=["T1"])
                add("pool", lambda e: e.tensor_tensor(out=g8(T1), in0=g8(T1), in1=bc_last(SGB, 64), op=ALU.add), reads=["T1", "SGB"], writes=["T1"])
                add("pool", lambda e: e.tensor_tensor(out=T1, in0=T1, in1=UF[u3], op=ALU.mult), reads=["T1", ("UF", u3)], writes=["T1"])

            def late1b(i):
                add("act", lambda e: e.activation(out=SQT, in_=T1, func=AF.Square), reads=["T1"], writes=["SQT"])
                add("dve", lambda e: e.tensor_reduce(out=SSQ8, in_=g8(SQT), axis=AX.X, op=ALU.add), reads=["SQT"], writes=["SSQ8"])

            def late1c(i):
                r = i % 2
                add("act", lambda e: e.activation(out=RT8, in_=SSQ8, func=AF.Sqrt, scale=1.0 / HD, bias=EPS), reads=["SSQ8"], writes=["RT8"])
                add("dve", lambda e: e.reciprocal(out=R8, in_=RT8), reads=["RT8"], writes=["R8"])
                add("dve", lambda e: e.tensor_tensor(out=g8(ON[r]), in0=g8(T1), in1=bc_last(R8, 64), op=ALU.mult), reads=["T1", "R8"], writes=[("ON", r)])

            def late2(i):
                r = i % 2
                tok = slice(i * 128, (i + 1) * 128)
                def trs(e):
                    ins = None
                    for p in range(4):
                        ins = e.transpose(out=TS[:, p, :], in_=ON[r][:, p * 128:(p + 1) * 128], identity=IDENT)
                    return ins
                add("pe", trs, reads=[("ON", r), "IDENT"], writes=["B7"])
                add("act", lambda e: e.activation(out=MIXT[:, 4:8, tok], in_=TS, func=AF.Copy, scale=GAINS[:, 4:5]),
                    reads=["B7"] + GAINS_R, writes=["MIXT"])

            load_x(0)
            load_x(1)
            front0(0)
            tp(0)
            load_x(2)
            for i in range(NT + 3):
                if i < NT:
                    proj(i)
                    front2(i)
                if i + 1 < NT:
                    tp(i + 1)
                if 0 <= i - 2 < NT:
                    early_b(i - 2)
                    late1a(i - 2)
                if i < NT:
                    early_a1(i)
                if i + 1 < NT:
                    front0a(i + 1)
                    if i + 3 < NT:
                        load_x(i + 3)
                if 0 <= i - 2 < NT:
                    late1b(i - 2)
                if i < NT:
                    early_a2(i)
                if i + 1 < NT:
                    front0b(i + 1)
                if 0 <= i - 2 < NT:
                    late1c(i - 2)
                if 0 <= i - 3 < NT:
                    late2(i - 3)

        def phase_b(b):
            if b == 0:
                scr_casts()
            for k in range(2):
                add("pool", lambda e, k=k: e.memset(QP[k], 0.0), writes=[("QP", k)])
            steps = []
            for p in range(4):
                for g in range(4):
                    top = 4 * g + 3
                    for kb in range(top, -1, -1):
                        steps.append((p, g, kb))
            ns = len(steps)
            Z = [ps_t[:, zb * 1024:(zb + 1) * 1024].rearrange("p (h t) -> p h t", h=2) for zb in range(3)]
            OB = bank(6)
            SSB = bank(7)

            def info(n):
                p, g, kb = steps[n]
                j = kb - 4 * g
                c0 = max(0, j) * 128
                return p, g, kb, j, c0

            def cols_of(n):
                return slice(info(n)[4], 512)

            def st_z(n):
                p, g, kb, j, c0 = info(n)
                cs = slice(c0, 512)
                qb = g % 2
                if kb == 4 * g + 3:
                    t0 = g * 512
                    add("pool", lambda e: e.tensor_copy(out=QP[qb][0:64, 0, :], in_=QT[0:64, p, t0:t0 + 512]), reads=["QT"], writes=[("QP", qb)])
                    add("pool", lambda e: e.tensor_copy(out=QP[qb][64:128, 1, :], in_=QT[64:128, p, t0:t0 + 512]), reads=["QT", ("QP", qb)], writes=[("QP", qb)])
                zb = n % 3
                def mm(e):
                    ins = None
                    for hh in range(2):
                        ins = e.matmul(Z[zb][:, hh, cs], lhsT=KT[:, p, kb * 128:(kb + 1) * 128], rhs=QP[qb][:, hh, cs], start=True, stop=True)
                    return ins
                add("pe", mm, reads=["KT", ("QP", qb)], writes=[("Z", zb)])

            def st_exp1(n):
                cs = cols_of(n)
                zb = n % 3; eb = n % 2
                add("act", lambda e: e.activation(out=E[eb][:, :, cs], in_=Z[zb][:, :, cs], func=AF.Exp), reads=[("Z", zb)], writes=[("E", eb)])

            def st_ln(n):
                p, g, kb, j, c0 = info(n)
                cs = slice(c0, 512)
                eb = n % 2; lb = n % 3
                add("act", lambda e: e.activation(out=L[lb][:, :, cs], in_=E[eb][:, :, cs], func=AF.Ln, bias=1.0), reads=[("E", eb)], writes=[("L", lb)])
                if j >= 0:
                    dc = slice(c0, c0 + 128)
                    add("dve", lambda e: e.tensor_tensor(out=L[lb][:, :, dc], in0=L[lb][:, :, dc], in1=bc_mid(MASK, 2), op=ALU.mult),
                        reads=[("L", lb), "MASK"], writes=[("L", lb)])

            def st_carry(n):
                p, g, kb, j, c0 = info(n)
                cs = slice(c0, 512)
                lb = n % 3
                if kb >= 2:
                    if kb == 4 * g + 3:
                        add("dve", lambda e: e.memset(LS, 0.0), writes=["LS", "LS1"])
                    add("pool", lambda e: e.tensor_tensor(out=LS[:, :, cs], in0=LS[:, :, cs], in1=L[lb][:, :, cs], op=ALU.add),
                        reads=["LS", ("L", lb)], writes=["LS"])

            def st_cast(n):
                p, g, kb, j, c0 = info(n)
                if kb >= 2:
                    ncs = cols_of(n + 2)
                    sb = n % 2
                    add("dve", lambda e: e.tensor_copy(out=LSB[sb][:, :, ncs], in_=LS[:, :, ncs]), reads=["LS", "LS1"], writes=[("LSB", sb)])

            def st_pe2(n):
                p, g, kb, j, c0 = info(n)
                cs = slice(c0, 512)
                zb = n % 3; lb = n % 3; sb = n % 2
                plb = (n - 1) % 3
                m = 4 * g + 3 - kb
                pcs = cols_of(n - 1) if m >= 1 else None
                def mm(e):
                    ins = None
                    for hh in range(2):
                        if m >= 2:
                            e.matmul(Z[zb][:, hh, cs], lhsT=NEGONES, rhs=LSB[sb][:, hh, cs], start=False, stop=False, skip_group_check=True)
                        if m >= 1:
                            e.matmul(Z[zb][:, hh, pcs], lhsT=NEGONES, rhs=L[plb][:, hh, pcs], start=False, stop=False, skip_group_check=True)
                        ins = e.matmul(Z[zb][:, hh, cs], lhsT=NEGTRI, rhs=L[lb][:, hh, cs], start=False, stop=True, skip_group_check=True)
                    return ins
                rd = [("L", lb), "NEGTRI", "NEGONES"]
                if m >= 1:
                    rd.append(("L", plb))
                if m >= 2:
                    rd.append(("LSB", sb))
                add("pe", mm, reads=rd, writes=[("Z", zb)])
                st_carry(n)

            def st_exp3(n):
                p, g, kb, j, c0 = info(n)
                cs = slice(c0, 512)
                zb = n % 3; ab = n % 2
                add("act", lambda e: e.activation(out=A[ab][:, :, cs], in_=Z[zb][:, :, cs], func=AF.Exp), reads=[("Z", zb)], writes=[("A", ab)])
                if j >= 0:
                    dc = slice(c0, c0 + 128)
                    add("dve", lambda e: e.tensor_tensor(out=A[ab][:, :, dc], in0=A[ab][:, :, dc], in1=bc_mid(MASK, 2), op=ALU.mult),
                        reads=[("A", ab), "MASK"], writes=[("A", ab)])

            def st_av(n):
                p, g, kb, j, c0 = info(n)
                cs = slice(c0, 512)
                ab = n % 2
                first = (kb == 4 * g + 3)
                last = (kb == 0)
                def mm(e):
                    ins = None
                    for hh in range(2):
                        ins = e.matmul(OB[:, cs], lhsT=VPAD[:, kb, p, hh, :], rhs=A[ab][:, hh, cs], start=(first and hh == 0),
                                       stop=(last and hh == 1), skip_group_check=True)
                    return ins
                add("pe", mm, reads=["VPAD", ("A", ab)], writes=["OB"])
                if last:
                    t0 = g * 512
                    add("dve", lambda e: e.tensor_copy(out=OSB, in_=OB), reads=["OB"], writes=["OSB"])
                    add("pool", lambda e: e.tensor_tensor(out=SQB, in0=OSB, in1=OSB, op=ALU.mult), reads=["OSB"], writes=["SQB"])
                    pending.setdefault(n + 1, []).append(
                        lambda: add("pe", lambda e: e.matmul(SSB, lhsT=BDONES, rhs=SQB, start=True, stop=True), reads=["SQB", "BDONES"], writes=["SSB"]))
                    def act_part():
                        add("act", lambda e: e.activation(out=R, in_=SSB, func=AF.Ln, scale=1.0 / HD, bias=EPS), reads=["SSB"], writes=["R"])
                        add("act", lambda e: e.activation(out=R, in_=R, func=AF.Exp, scale=-0.5), reads=["R"], writes=["R"])
                    pending.setdefault(n + 2, []).append(act_part)
                    pending.setdefault(n + 3, []).append(
                        lambda: add("dve", lambda e: e.scalar_tensor_tensor(out=MIXT[:, p, t0:t0 + 512], in0=OSB, scalar=GAINS[:, 3:4], in1=R,
                                                                            op0=ALU.mult, op1=ALU.mult), reads=["OSB", "R"] + GAINS_R, writes=["MIXT"]))

            pending = {}

            for n0 in range(3):
                st_z(n0)
            st_exp1(0)
            for n in range(-1, ns):
                if n + 1 < ns:
                    st_ln(n + 1)
                    st_pe2(n + 1)
                if n >= 0:
                    st_exp3(n)
                    for pc in pending.pop(n, []):
                        pc()
                    st_av(n)
                    if n + 3 < ns:
                        st_z(n + 3)
                if n + 2 < ns:
                    st_exp1(n + 2)
                if n + 1 < ns:
                    st_cast(n + 1)
            for kk in sorted(pending):
                for pc in pending[kk]:
                    pc()

        def phase_c(b):
            HO = [bank(0, 2), bank(2, 2)]
            TPH = bank_bf(4).rearrange("p (c t) -> p c t", c=8)
            FB = [bank(4), bank(5)]
            O2 = [bank(6)[:, 0:256], bank(7)[:, 0:256]]
            s1v = scr1.rearrange("(c p) f -> p c f", p=128)
            s2v = scr2.rearrange("(c p) m -> p c m", p=128)

            def load_xr(T, sub):
                r = sub % 2
                tok0 = T * 512 + sub * 128
                add("sp", lambda e: e.dma_start(out=XR[r], in_=x[b, tok0:tok0 + 128, :]), writes=[("XR", r)], dma="XR%d" % r)

            def load_w1(fg):
                r = fg % 3
                add("act", lambda e: e.dma_start(out=W1[r], in_=s1v[:, :, fg * 512:(fg + 1) * 512]), reads=SCR1_R, writes=[("W1", r)], dma="W1%d" % r)

            def load_w2(q):
                r = q % 2
                add("sp", lambda e: e.dma_start(out=W2[r], in_=s2v[:, :, q * 256:(q + 1) * 256]), reads=SCR2_R, writes=[("W2", r)], dma="W2%d" % r)

            def c1_mm(T, sub):
                hb = sub % 2
                tok = slice(T * 512 + sub * 128, T * 512 + (sub + 1) * 128)
                def mm(e):
                    ins = None
                    for half in range(2):
                        for c in range(8):
                            ins = e.matmul(HO[hb][:, half * 512:(half + 1) * 512], lhsT=MIXT[:, c, tok], rhs=WOUT[:, c, half * 512:(half + 1) * 512],
                                           start=(c == 0), stop=(c == 7))
                    return ins
                add("pe", mm, reads=["MIXT"] + WOUT_R, writes=[("HO", hb)])

            def c1_mid(T, sub):
                hb = sub % 2
                r = sub % 2
                tb = T % 2
                add("dve", lambda e: e.tensor_tensor(out=H1[tb][:, sub, :], in0=HO[hb], in1=XR[r], op=ALU.add),
                    reads=[("HO", hb), ("XR", r)], writes=[("H1", tb, sub)])
                if sub + 2 < 4:
                    load_xr(T, sub + 2)
                add("act", lambda e: e.activation(out=H1B, in_=H1[tb][:, sub, :], func=AF.Copy), reads=[("H1", tb, sub)], writes=["H1B"])
                add("act", lambda e: e.activation(out=JUNKC, in_=H1[tb][:, sub, :], func=AF.Square, accum_out=SS2[:, sub:sub + 1]),
                    reads=[("H1", tb, sub)], writes=[("SS2", sub)])
                add("act", lambda e: e.activation(out=RT2[:, sub:sub + 1], in_=SS2[:, sub:sub + 1], func=AF.Sqrt, scale=1.0 / D, bias=EPS),
                    reads=[("SS2", sub)], writes=[("RT2", sub)])
                add("dve", lambda e: e.reciprocal(out=R2[:, sub:sub + 1], in_=RT2[:, sub:sub + 1]), reads=[("RT2", sub)], writes=[("R2", sub)])
                add("dve", lambda e: e.tensor_tensor(out=R2SQ[tb][:, sub:sub + 1], in0=R2[:, sub:sub + 1], in1=R2[:, sub:sub + 1], op=ALU.mult),
                    reads=[("R2", sub)], writes=[("R2SQ", tb, sub)])

            def c1_tr(T, sub):
                tb = T % 2
                def tr(e):
                    ins = None
                    for c in range(8):
                        ins = e.transpose(out=TPH[:, c, :], in_=H1B[:, c * 128:(c + 1) * 128], identity=IDENT)
                    return ins
                add("pe", tr, reads=["H1B", "IDENT"], writes=[("FB", 0)])
                add("dve", lambda e: e.tensor_tensor(out=H1T[tb][:, :, sub * 128:(sub + 1) * 128], in0=TPH, in1=bc_last(G2T, 128), op=ALU.mult),
                    reads=[("FB", 0), "G2T"], writes=[("H1T", tb)])

            def c1_pieces(T):
                return {
                    0: [lambda: (load_xr(T, 0), load_xr(T, 1))],
                    1: [lambda: c1_mm(T, 0)],
                    2: [lambda: c1_mm(T, 1)],
                    3: [lambda: c1_mid(T, 0)],
                    4: [lambda: c1_mm(T, 2)],
                    5: [lambda: c1_tr(T, 0), lambda: c1_mid(T, 1)],
                    6: [lambda: c1_mm(T, 3)],
                    7: [lambda: c1_tr(T, 1), lambda: c1_mid(T, 2)],
                    9: [lambda: c1_tr(T, 2), lambda: c1_mid(T, 3)],
                    11: [lambda: c1_tr(T, 3)],
                }

            def c2(T):
                tb = T % 2
                for fg in range(8):
                    r = fg % 3
                    for fj in range(4):
                        fc = fg * 4 + fj
                        fb = fc % 2
                        def mm(e, r=r, fj=fj, fb=fb):
                            ins = None
                            for c in range(8):
                                ins = e.matmul(FB[fb], lhsT=W1[r][:, c, fj * 128:(fj + 1) * 128], rhs=H1T[tb][:, c, :], start=(c == 0), stop=(c == 7))
                            return ins
                        add("pe", mm, reads=[("W1", r), ("H1T", tb)], writes=[("FB", fb)])
                        add("act", lambda e, fb=fb: e.activation(out=RL[fb], in_=FB[fb], func=AF.Relu), reads=[("FB", fb)], writes=[("RL", fb)])
                        add("dve", lambda e, fb=fb, fc=fc: e.tensor_tensor(out=AFF[:, fc, :], in0=RL[fb], in1=RL[fb], op=ALU.mult),
                            reads=[("RL", fb)], writes=["AFF"])
                    if fg + 3 < 8:
                        load_w1(fg + 3)

            def c3(T, pieces):
                tb = T % 2
                k = 0
                for q in range(4):
                    r = q % 2
                    for sub in range(4):
                        ob = (q * 4 + sub) % 2
                        def mm(e, r=r, sub=sub, ob=ob):
                            ins = None
                            for fc in range(32):
                                ins = e.matmul(O2[ob], lhsT=AFF[:, fc, sub * 128:(sub + 1) * 128], rhs=W2[r][:, fc, :], start=(fc == 0), stop=(fc == 31))
                            return ins
                        add("pe", mm, reads=["AFF", ("W2", r)], writes=[("O2", ob)])
                        qs = slice(q * 256, (q + 1) * 256)
                        add("dve", lambda e, sub=sub, ob=ob, qs=qs: e.scalar_tensor_tensor(out=H1[tb][:, sub, qs], in0=O2[ob], scalar=R2SQ[tb][:, sub:sub + 1],
                                                                                         in1=H1[tb][:, sub, qs], op0=ALU.mult, op1=ALU.add),
                            reads=[("O2", ob), ("R2SQ", tb, sub), ("H1", tb, sub)], writes=[("H1", tb, sub), ("O2", ob)])
                        for pc in pieces.get(k, []):
                            pc()
                        k += 1
                    if q + 2 < 4:
                        load_w2(q + 2)
                    elif T + 1 < 4:
                        load_w2(q - 2)
                for sub in range(4):
                    tok0 = T * 512 + sub * 128
                    add("sp", lambda e, sub=sub, tok0=tok0: e.dma_start(out=out[b, tok0:tok0 + 128, :], in_=H1[tb][:, sub, :]),
                        reads=[("H1", tb, sub)], writes=["OUT"], dma="out%d" % sub)

            load_w1(0)
            load_w1(1)
            load_w1(2)
            p0 = c1_pieces(0)
            for kk in sorted(p0):
                for pc in p0[kk]:
                    pc()
            load_w2(0)
            load_w2(1)
            for T in range(4):
                c2(T)
                if T == 3 and b + 1 < NB:
                    win_load(b + 1, alias=True)
                if T + 1 < 4:
                    load_w1(0)
                    load_w1(1)
                    load_w1(2)
                c3(T, c1_pieces(T + 1) if T + 1 < 4 else {})

        setup()
        win_load(0)
        for b in range(NB):
            phase_a(b)
            SC.barrier()
            phase_b(b)
            SC.barrier()
            phase_c(b)
            SC.barrier()

        eng_sems = {e: es.enter_context(nc.semaphore("s_" + e)) for e in ENGS}
        dma_sems = {s: es.enter_context(nc.semaphore("d_" + s)) for s in SC.dma_slots()}
        SC.assign_tokens(eng_sems, dma_sems)

        block = es.enter_context(nc.Block())

        @block.tensor
        def _(e):
            SC.emit("pe", e)

        @block.scalar
        def _(e):
            SC.emit("act", e)

        @block.vector
        def _(e):
            SC.emit("dve", e)

        @block.gpsimd
        def _(e):
            SC.emit("pool", e)

        @block.sync
        def _(e):
            SC.emit("sp", e)

    return nc


_NC = None


def kernel(x, norm1_g, w_in, q_norm_g, k_norm_g, sg_v_norm_g, sg_w, sg_b,
           sb_out_norm_g, sg_out_norm_g, w_out, norm2_g, w_ff1, w_ff2):
    global _NC
    if _NC is None:
        _NC = build_nc()
    f = lambda a: np.ascontiguousarray(np.asarray(a, dtype=np.float32))
    x = f(x)
    shared = {
        "norm1_g": f(norm1_g), "w_in": f(w_in), "q_norm_g": f(q_norm_g), "k_norm_g": f(k_norm_g),
        "sg_v_norm_g": f(sg_v_norm_g), "sg_w": f(sg_w), "sg_b": f(sg_b), "sb_out_norm_g": f(sb_out_norm_g),
        "sg_out_norm_g": f(sg_out_norm_g), "w_out": f(w_out), "norm2_g": f(norm2_g), "w_ff1": f(w_ff1), "w_ff2": f(w_ff2),
    }
    in_maps = []
    for c in range(NCORES):
        m = dict(shared)
        m["x"] = np.ascontiguousarray(x[c * NB:(c + 1) * NB])
        in_maps.append(m)
    res = run_bass_kernel_spmd(_NC, in_maps, core_ids=list(range(NCORES)))
    outs = [np.asarray(res.results[c]["out"], dtype=np.float32) for c in range(NCORES)]
    return np.concatenate(outs, axis=0)
```

```python
import numpy as np
from contextlib import ExitStack
import concourse.bass as bass
import concourse.mybir as mybir
from concourse.bass_utils import run_bass_kernel_spmd

F32 = mybir.dt.float32
BF16 = mybir.dt.bfloat16
U8 = mybir.dt.uint8
AF = mybir.ActivationFunctionType
ALU = mybir.AluOpType
AX = mybir.AxisListType

NCORES = 8
NB = 2
S = 2048
D = 1024
HD = 64
DFF = 4096
INW = 2560
EPS = 1e-6
NT = S // 128

ENGS = ("pe", "act", "dve", "pool", "sp")


class Op:
    __slots__ = ("eng", "fn", "deps", "needs_inc", "token", "dma", "idx")

    def __init__(self, eng, fn, dma):
        self.eng = eng
        self.fn = fn
        self.deps = []
        self.needs_inc = False
        self.token = None
        self.dma = dma


class Sched:
    def __init__(self):
        self.ops = {e: [] for e in ENGS}
        self.res = {}
        self.last_real = {e: None for e in ENGS}
        self.last_compute = {e: None for e in ENGS}
        self.dmas_since_barrier = []
        self.group_slots = set()

    def add(self, eng, fn, reads=(), writes=(), dma=None, group=False):
        op = Op(eng, fn, dma)
        deps = []
        for r in reads:
            st = self.res.get(r)
            if st is not None and st[0] is not None:
                deps.append(st[0])
        for w in writes:
            st = self.res.get(w)
            if st is not None:
                if st[0] is not None:
                    deps.append(st[0])
                deps.extend(st[1])
        seen = set()
        for d in deps:
            if id(d) in seen:
                continue
            seen.add(id(d))
            if d.eng == "pe" and eng == "pe" and d.dma is None:
                continue
            op.deps.append(d)
            d.needs_inc = True
        for r in reads:
            st = self.res.setdefault(r, [None, []])
            st[1].append(op)
        for w in writes:
            self.res[w] = [op, []]
        op.idx = len(self.ops[eng])
        self.ops[eng].append(op)
        if fn is not None:
            self.last_real[eng] = op
            if dma is None:
                self.last_compute[eng] = op
        if dma is not None:
            self.dmas_since_barrier.append(op)
            if group:
                self.group_slots.add(dma)
        return op

    def barrier(self):
        lasts = dict(self.last_compute)
        dmas = list(self.dmas_since_barrier)
        self.dmas_since_barrier = []
        for e in ENGS:
            op = Op(e, None, None)
            op.idx = len(self.ops[e])
            for e2 in ENGS:
                l = lasts[e2]
                if l is not None and e2 != e and l.dma is None:
                    op.deps.append(l)
                    l.needs_inc = True
            for d in dmas:
                op.deps.append(d)
            self.ops[e].append(op)

    def assign_tokens(self, eng_sems, dma_sems):
        slot_cnt = {}
        slot_ops = {}
        for e in ENGS:
            cnt = 0
            for op in self.ops[e]:
                if op.dma is not None:
                    c = slot_cnt.get(op.dma, 0) + 16
                    slot_cnt[op.dma] = c
                    op.token = (dma_sems[op.dma], c)
                    slot_ops.setdefault(op.dma, []).append(op)
                elif op.needs_inc:
                    cnt += 1
                    op.token = (eng_sems[e], cnt)
        for slot in self.group_slots:
            for op in slot_ops[slot]:
                op.token = (dma_sems[slot], slot_cnt[slot])

    def dma_slots(self):
        s = []
        for e in ENGS:
            for op in self.ops[e]:
                if op.dma is not None and op.dma not in s:
                    s.append(op.dma)
        return s

    def emit(self, eng, e):
        waited = {}
        for op in self.ops[eng]:
            need = {}
            for d in op.deps:
                sem, val = d.token
                k = id(sem)
                if k not in need or need[k][1] < val:
                    need[k] = (sem, val)
            for k, (sem, val) in need.items():
                if waited.get(k, 0) >= val:
                    continue
                e.wait_ge(sem, val)
                waited[k] = val
            if op.fn is not None:
                ins = op.fn(e)
                if op.dma is not None:
                    ins.then_inc(op.token[0], 16)
                elif op.needs_inc:
                    ins.then_inc(op.token[0], 1)


class Arena:
    def __init__(self, ap, limit):
        self.ap = ap
        self.limit = limit

    def view(self, off, nbytes, dt, pattern=None, **kw):
        assert off % 4 == 0 and off + nbytes <= self.limit, (off, nbytes, self.limit)
        v = self.ap[:, off:off + nbytes].bitcast(dt)
        if pattern is not None:
            v = v.rearrange(pattern, **kw)
        return v


def bc_last(ap2, n):
    return ap2.unsqueeze(2).to_broadcast([ap2.shape[0], ap2.shape[1], n])


def bc_mid(ap2, n):
    return ap2.unsqueeze(1).to_broadcast([ap2.shape[0], n, ap2.shape[1]])


def build_nc():
    nc = bass.Bass("TRN2", target_bir_lowering=False)
    dt = nc.dram_tensor
    x = dt("x", [NB, S, D], F32, kind="ExternalInput").ap()
    norm1_g = dt("norm1_g", [D], F32, kind="ExternalInput").ap()
    w_in = dt("w_in", [D, INW], F32, kind="ExternalInput").ap()
    q_norm_g = dt("q_norm_g", [HD], F32, kind="ExternalInput").ap()
    k_norm_g = dt("k_norm_g", [HD], F32, kind="ExternalInput").ap()
    sg_v_norm_g = dt("sg_v_norm_g", [HD], F32, kind="ExternalInput").ap()
    sg_w = dt("sg_w", [8, 128, 128], F32, kind="ExternalInput").ap()
    sg_b = dt("sg_b", [8, 128], F32, kind="ExternalInput").ap()
    sb_out_norm_g = dt("sb_out_norm_g", [HD], F32, kind="ExternalInput").ap()
    sg_out_norm_g = dt("sg_out_norm_g", [HD], F32, kind="ExternalInput").ap()
    w_out = dt("w_out", [D, D], F32, kind="ExternalInput").ap()
    norm2_g = dt("norm2_g", [D], F32, kind="ExternalInput").ap()
    w_ff1 = dt("w_ff1", [D, DFF], F32, kind="ExternalInput").ap()
    w_ff2 = dt("w_ff2", [DFF, D], F32, kind="ExternalInput").ap()
    out = dt("out", [NB, S, D], F32, kind="ExternalOutput").ap()
    scr1 = dt("scr1", [D, DFF], BF16).ap()
    scr2 = dt("scr2", [DFF, D], BF16).ap()

    SC = Sched()
    es = ExitStack()
    with es:
        ARENA_BYTES = 207 * 1024
        arena_t = es.enter_context(nc.sbuf_tensor("arena", [128, ARENA_BYTES], U8))
        ps_t = es.enter_context(nc.psum_tensor("ps", [128, 4096], F32))
        AR = Arena(arena_t, ARENA_BYTES)

        def bank(k, n=1):
            return ps_t[:, k * 512:(k + n) * 512]

        def bank_bf(k):
            return ps_t[:, k * 512:(k + 1) * 512].bitcast(BF16)

        off = [0]

        def take(nbytes):
            o = off[0]
            off[0] += (nbytes + 31) // 32 * 32
            return o

        KB = 1024
        o_ident32 = take(512)
        o_ident = take(256); o_negtri = take(256); o_negones = take(256); o_bdones = take(256); o_mask = take(256)
        o_wt = take(2 * KB); o_sgb = take(32); o_gains = take(64)
        o_g1t = take(32); o_g2t = take(32); o_gvbc = take(256)
        o_small = take(1 * KB)
        o_wout = take(16 * KB)
        o_mixt = take(32 * KB)
        base_state = off[0]
        o_qt = take(16 * KB); o_kt = take(16 * KB); o_vpad = take(32 * KB)
        base_ab = off[0]
        o_win = take(40 * KB)
        o_x = [take(4 * KB), take(4 * KB)]
        o_xb = take(2 * KB)
        o_xt = [take(2 * KB), take(2 * KB)]
        o_qf = take(2 * KB); o_kf = take(2 * KB); o_vsf = take(2 * KB); o_uf = [take(2 * KB), take(2 * KB), take(2 * KB)]
        o_sq = take(6 * KB)
        o_stgf = o_sq; o_stgb = o_sq + 4 * KB
        o_n3 = [take(3 * KB), take(3 * KB)]; o_on = [take(1 * KB), take(1 * KB)]
        o_junk = take(2 * KB)
        o_t1 = take(2 * KB)
        end_a = off[0]
        off[0] = base_ab
        o_e = [take(4 * KB), take(4 * KB)]
        o_l = [take(2 * KB), take(2 * KB), take(2 * KB)]
        o_ls = take(4 * KB)
        o_lsb = [take(2 * KB), take(2 * KB)]
        o_a = [take(2 * KB), take(2 * KB)]
        o_qp = [take(2 * KB), take(2 * KB)]
        o_osb = take(2 * KB); o_sqb = take(1 * KB); o_r = take(2 * KB)
        end_b = off[0]
        off[0] = base_state
        o_af = take(32 * KB); o_h1 = [take(16 * KB), take(16 * KB)]
        assert off[0] == o_win, (off[0], o_win)
        o_w1 = [take(8 * KB), take(8 * KB), take(8 * KB)]
        o_h1t = [take(8 * KB), take(8 * KB)]
        assert off[0] == o_win + 40 * KB
        o_xr = [take(4 * KB), take(4 * KB)]
        o_h1b = take(2 * KB)
        o_junkc = take(2 * KB)
        o_rl = [take(1 * KB), take(1 * KB)]
        o_w2 = [take(16 * KB), take(16 * KB)]
        end_c = off[0]
        assert max(end_a, end_b, end_c) <= ARENA_BYTES, (end_a, end_b, end_c)

        V = AR.view
        IDENT32 = V(o_ident32, 512, F32)
        IDENT = V(o_ident, 256, BF16); NEGTRI = V(o_negtri, 256, BF16); NEGONES = V(o_negones, 256, BF16)
        BDONES = V(o_bdones, 256, BF16); MASK = V(o_mask, 256, BF16)
        WT = V(o_wt, 2 * KB, BF16, "p (g t) -> p g t", g=8)
        SGB = V(o_sgb, 32, F32)
        GAINS = V(o_gains, 64, F32)
        G1T = V(o_g1t, 32, F32); G2T = V(o_g2t, 32, F32); GVBC = V(o_gvbc, 256, F32)
        SMALL = V(o_small, 1 * KB, F32)
        STGF = V(o_stgf, 4 * KB, F32, "p (g s) -> p g s", g=8)
        STGB = V(o_stgb, 2 * KB, BF16, "p (g s) -> p g s", g=8)
        WOUT = V(o_wout, 16 * KB, BF16, "p (c f) -> p c f", c=8)
        MIXT = V(o_mixt, 32 * KB, BF16, "p (c t) -> p c t", c=8)
        QT = V(o_qt, 16 * KB, BF16, "p (c t) -> p c t", c=4)
        KT = V(o_kt, 16 * KB, BF16, "p (c t) -> p c t", c=4)
        VPAD = V(o_vpad, 32 * KB, BF16, "p (i c h d) -> p i c h d", i=16, c=4, h=2)
        VPAD_FLAT = V(o_vpad, 32 * KB, BF16)
        WIN = V(o_win, 40 * KB, BF16, "p (c f) -> p c f", c=8)
        X = [V(o, 4 * KB, F32) for o in o_x]
        SQT = V(o_xb, 2 * KB, F32)
        XT = [V(o, 2 * KB, BF16, "p (c t) -> p c t", c=8) for o in o_xt]
        XTF = [V(o, 2 * KB, BF16) for o in o_xt]
        QF = V(o_qf, 2 * KB, F32); KF = V(o_kf, 2 * KB, F32); VSF = V(o_vsf, 2 * KB, F32); UF = [V(o, 2 * KB, F32) for o in o_uf]
        SQ3 = V(o_sq, 6 * KB, F32)
        SQ = V(o_sq, 2 * KB, F32)
        F3 = V(o_qf, 6 * KB, F32)
        N3 = [V(o, 3 * KB, BF16) for o in o_n3]
        JUNK = V(o_junk, 2 * KB, BF16)
        ON = [V(o, 1 * KB, BF16) for o in o_on]
        T1 = V(o_t1, 2 * KB, F32)
        E = [V(o, 4 * KB, F32, "p (h t) -> p h t", h=2) for o in o_e]
        L = [V(o, 2 * KB, BF16, "p (h t) -> p h t", h=2) for o in o_l]
        LS = V(o_ls, 4 * KB, F32, "p (h t) -> p h t", h=2)
        LSB = [V(o, 2 * KB, BF16, "p (h t) -> p h t", h=2) for o in o_lsb]
        A = [V(o, 2 * KB, BF16, "p (h t) -> p h t", h=2) for o in o_a]
        QP = [V(o, 2 * KB, BF16, "p (h t) -> p h t", h=2) for o in o_qp]
        OSB = V(o_osb, 2 * KB, F32); SQB = V(o_sqb, 1 * KB, BF16); R = V(o_r, 2 * KB, F32)
        AFF = V(o_af, 32 * KB, BF16, "p (c t) -> p c t", c=32)
        H1T = [V(o, 8 * KB, BF16, "p (c t) -> p c t", c=8) for o in o_h1t]
        H1 = [V(o, 16 * KB, F32, "p (s f) -> p s f", s=4) for o in o_h1]
        XR = [V(o, 4 * KB, F32) for o in o_xr]
        H1B = V(o_h1b, 2 * KB, BF16)
        JUNKC = V(o_junkc, 2 * KB, BF16)
        RL = [V(o, 1 * KB, BF16) for o in o_rl]
        W1 = [V(o, 8 * KB, BF16, "p (c f) -> p c f", c=8) for o in o_w1]
        W2 = [V(o, 16 * KB, BF16, "p (c f) -> p c f", c=32) for o in o_w2]

        sm = [0]

        def small(n):
            c = sm[0]
            sm[0] += n
            assert sm[0] <= 256
            return SMALL[:, c:c + n]

        SS1 = [small(1), small(1)]
        RT1 = [small(1), small(1)]
        RSTD1 = [small(1), small(1)]
        SSQ8 = small(8); RT8 = small(8); R8 = small(8)
        SSQ24 = small(24); RT24 = small(24); R24 = small(24)
        SS2 = small(4); RT2 = small(4); R2 = small(4); R2SQ = [small(4), small(4)]

        add = SC.add
        GAINS_R = [("GAINS", col, hh) for col in (0, 1, 3, 4) for hh in range(2)]
        WIN_R = [("WIN", c) for c in range(8)]
        WOUT_R = [("WOUT", c) for c in range(8)]
        SCR1_R = [("scr1", i) for i in range(4)]
        SCR2_R = [("scr2", i) for i in range(4)]

        def scr_casts():
            for i in range(4):
                add("pool", lambda e, i=i: e.dma_start(out=scr1[i * 256:(i + 1) * 256, :], in_=w_ff1[i * 256:(i + 1) * 256, :]),
                    writes=[("scr1", i)], dma="scr", group=True)
            for i in range(4):
                add("pool", lambda e, i=i: e.dma_start(out=scr2[i * 1024:(i + 1) * 1024, :], in_=w_ff2[i * 1024:(i + 1) * 1024, :]),
                    writes=[("scr2", i)], dma="scr", group=True)

        def wout_load():
            for c in range(8):
                add("pool", lambda e, c=c: e.dma_start(out=WOUT[:, c, :], in_=w_out[c * 128:(c + 1) * 128, :]),
                    writes=[("WOUT", c)], dma="wout", group=True)

        def setup():
            add("sp", lambda e: e.dma_start(out=G1T, in_=norm1_g.rearrange("(c p) -> p c", p=128), allow_slow_non_contiguous=True),
                writes=["G1T"], dma="g1t")
            add("act", lambda e: e.dma_start(out=STGF, in_=sg_w.rearrange("g t s -> t g s")), writes=["SQ"], dma="const", group=True)
            add("act", lambda e: e.dma_start(out=GVBC, in_=sg_v_norm_g.partition_broadcast(128)), writes=["GVBC"], dma="const", group=True)
            add("act", lambda e: e.dma_start(out=SGB, in_=sg_b.rearrange("g t -> t g"), allow_slow_non_contiguous=True),
                writes=["SGB"], dma="const", group=True)
            add("act", lambda e: e.dma_start(out=G2T, in_=norm2_g.rearrange("(c p) -> p c", p=128), allow_slow_non_contiguous=True),
                writes=["G2T"], dma="const", group=True)
            for col, gsrc in ((0, q_norm_g), (1, k_norm_g), (3, sb_out_norm_g), (4, sg_out_norm_g)):
                for hh in range(2):
                    add("act", lambda e, col=col, gsrc=gsrc, hh=hh: e.dma_start(
                        out=GAINS[hh * 64:(hh + 1) * 64, col:col + 1], in_=gsrc.rearrange("(d o) -> d o", o=1)),
                        writes=[("GAINS", col, hh)], dma="const", group=True)
            add("pool", lambda e: e.memset(IDENT32, 1.0), writes=["IDENT32"])
            add("pool", lambda e: e.affine_select(out=IDENT32, in_=IDENT32, pattern=[[1, 128]], compare_op=ALU.is_equal, fill=0.0, base=0, channel_multiplier=-1),
                reads=["IDENT32"], writes=["IDENT32"])
            add("pool", lambda e: e.memset(IDENT, 1.0), writes=["IDENT"])
            add("pool", lambda e: e.affine_select(out=IDENT, in_=IDENT, pattern=[[1, 128]], compare_op=ALU.is_equal, fill=0.0, base=0, channel_multiplier=-1),
                reads=["IDENT"], writes=["IDENT"])
            add("pool", lambda e: e.memset(NEGTRI, -1.0), writes=["NEGTRI"])
            add("pool", lambda e: e.affine_select(out=NEGTRI, in_=NEGTRI, pattern=[[-1, 128]], compare_op=ALU.is_ge, fill=0.0, base=0, channel_multiplier=1),
                reads=["NEGTRI"], writes=["NEGTRI"])
            add("pool", lambda e: e.memset(NEGONES, -1.0), writes=["NEGONES"])
            add("pool", lambda e: e.memset(BDONES, 0.0), writes=["BDONES"])
            add("pool", lambda e: e.memset(BDONES[0:64, 0:64], 1.0), reads=["BDONES"], writes=["BDONES"])
            add("pool", lambda e: e.memset(BDONES[64:128, 64:128], 1.0), reads=["BDONES"], writes=["BDONES"])
            add("pool", lambda e: e.memset(MASK, 1.0), writes=["MASK"])
            add("pool", lambda e: e.affine_select(out=MASK, in_=MASK, pattern=[[1, 128]], compare_op=ALU.is_gt, fill=0.0, base=0, channel_multiplier=-1),
                reads=["MASK"], writes=["MASK"])
            add("dve", lambda e: e.tensor_tensor(out=GAINS[:, 2:3], in0=GAINS[:, 0:1], in1=GAINS[:, 1:2], op=ALU.mult),
                reads=GAINS_R, writes=["GQK0"])
            add("dve", lambda e: e.tensor_scalar(out=GAINS[:, 2:3], in0=GAINS[:, 2:3], scalar1=0.125, scalar2=None, op0=ALU.mult),
                reads=["GQK0"], writes=["GQK"])
            add("pool", lambda e: e.tensor_copy(out=STGB, in_=STGF), reads=["SQ"], writes=["SQ"])
            tp = bank_bf(7).rearrange("p (g t) -> p g t", g=8)
            def tr_w(e):
                ins = None
                for g in range(8):
                    ins = e.transpose(out=tp[:, g, :], in_=STGB[:, g, :], identity=IDENT)
                return ins
            add("pe", tr_w, reads=["SQ", "IDENT"], writes=["B7"])
            add("dve", lambda e: e.tensor_copy(out=WT, in_=tp), reads=["B7"], writes=["WT0"])
            add("pool", lambda e: e.affine_select(out=WT, in_=WT, pattern=[[0, 8], [1, 128]], compare_op=ALU.is_ge, fill=0.0,
                                                  base=0, channel_multiplier=-1), reads=["WT0"], writes=["WT"])

        def win_load(b, alias=False):
            for c in range(8):
                wr = [("WIN", c)]
                if alias and c == 0:
                    wr += [("W1", 0), ("W1", 1), ("W1", 2), ("H1T", 0), ("H1T", 1)]
                add("pool", lambda e, c=c: e.dma_start(out=WIN[:, c, :], in_=w_in[c * 128:(c + 1) * 128, :]),
                    writes=wr, dma="win%d" % b, group=True)

        def phase_a(b):
            if b == 0:
                wout_load()
            P = [bank(j) for j in range(5)]
            TPX = bank_bf(5).rearrange("p (c t) -> p c t", c=8)
            TPXA = bank(5).rearrange("p (c t) -> p c t", c=4)
            TPXB = bank(7).rearrange("p (c t) -> p c t", c=4)
            Y = bank(2)
            TQK = bank_bf(6).rearrange("p (c t) -> p c t", c=8)
            TS = bank_bf(7)[:, 0:512].rearrange("p (c t) -> p c t", c=4)
            g8 = lambda ap: ap.rearrange("p (g d) -> p g d", d=64)

            def load_x(i):
                r = i % 2
                add("sp", lambda e: e.dma_start(out=X[r], in_=x[b, i * 128:(i + 1) * 128, :]), writes=[("X", r)], dma="X%d" % r)

            def front0a(i):
                r = i % 2
                add("act", lambda e: e.activation(out=JUNK, in_=X[r], func=AF.Square, accum_out=SS1[r]),
                    reads=[("X", r)], writes=[("SS1", r)])

            def front0b(i):
                r = i % 2
                add("act", lambda e: e.activation(out=RT1[r], in_=SS1[r], func=AF.Sqrt, scale=1.0 / D, bias=EPS),
                    reads=[("SS1", r)], writes=[("RT1", r)])
                add("dve", lambda e: e.reciprocal(out=RSTD1[r], in_=RT1[r]), reads=[("RT1", r)], writes=[("RSTD1", r)])

            def front0(i):
                front0a(i)
                front0b(i)

            def tp(i):
                r = i % 2
                def tra(e):
                    ins = None
                    for c in range(4):
                        ins = e.transpose(out=TPXA[:, c, :], in_=X[r][:, c * 128:(c + 1) * 128], identity=IDENT32)
                    return ins
                def trb(e):
                    ins = None
                    for c in range(4):
                        ins = e.transpose(out=TPXB[:, c, :], in_=X[r][:, (4 + c) * 128:(5 + c) * 128], identity=IDENT32)
                    return ins
                add("pe", tra, reads=[("X", r), "IDENT32"], writes=["B5"])
                add("pe", trb, reads=[("X", r), "IDENT32"], writes=["B7"])
                add("dve", lambda e: e.tensor_tensor(out=XT[r][:, 0:4, :], in0=TPXA, in1=bc_last(G1T[:, 0:4], 128), op=ALU.mult),
                    reads=["B5", "G1T"], writes=[("XT", r, 0)])
                add("dve", lambda e: e.tensor_tensor(out=XT[r][:, 4:8, :], in0=TPXB, in1=bc_last(G1T[:, 4:8], 128), op=ALU.mult),
                    reads=["B7", "G1T"], writes=[("XT", r, 1)])

            def proj(i):
                r = i % 2
                for j in range(5):
                    def mm(e, j=j):
                        ins = None
                        for c in range(8):
                            ins = e.matmul(P[j], lhsT=XT[r][:, c, :], rhs=WIN[:, c, j * 512:(j + 1) * 512], start=(c == 0), stop=(c == 7))
                        return ins
                    add("pe", mm, reads=[("XT", r, 0), ("XT", r, 1)] + WIN_R, writes=[("P", j)])

            def front2(i):
                r = i % 2
                sc = RSTD1[r]
                p2 = P[2].rearrange("p (c h d) -> p c h d", c=4, h=2)
                for hh in range(2):
                    add("pool", lambda e, hh=hh: e.memset(VPAD[:, i, :, hh, (1 - hh) * 64:(2 - hh) * 64], 0.0), writes=[("VPADZ", hh)])
                for hh in range(2):
                    add("act", lambda e, hh=hh: e.activation(out=VPAD[:, i, :, hh, hh * 64:(hh + 1) * 64], in_=p2[:, :, hh, :], func=AF.Copy, scale=sc),
                        reads=[("P", 2), ("RSTD1", r), "VPAD"], writes=[("VPADW", hh)])
                add("act", lambda e: e.activation(out=QF, in_=P[0], func=AF.Copy, scale=sc), reads=[("P", 0), ("RSTD1", r)], writes=["QF"])
                add("act", lambda e: e.activation(out=KF, in_=P[1], func=AF.Copy, scale=sc), reads=[("P", 1), ("RSTD1", r)], writes=["KF"])
                add("act", lambda e: e.activation(out=VSF, in_=P[4], func=AF.Copy, scale=sc), reads=[("P", 4), ("RSTD1", r)], writes=["VSF"])
                u3 = i % 3
                add("act", lambda e: e.activation(out=UF[u3], in_=P[3], func=AF.Copy, scale=sc), reads=[("P", 3), ("RSTD1", r)], writes=[("UF", u3)])

            def early_a1(i):
                add("act", lambda e: e.activation(out=SQ3, in_=F3, func=AF.Square), reads=["QF", "KF", "VSF"], writes=["SQ"])
                add("dve", lambda e: e.tensor_reduce(out=SSQ24, in_=g8(SQ3), axis=AX.X, op=ALU.add), reads=["SQ"], writes=["SSQ24"])

            def early_a2(i):
                add("act", lambda e: e.activation(out=RT24, in_=SSQ24, func=AF.Sqrt, scale=1.0 / HD, bias=EPS), reads=["SSQ24"], writes=["RT24"])
                add("dve", lambda e: e.reciprocal(out=R24, in_=RT24), reads=["RT24"], writes=["R24"])
                nb = i % 2
                add("dve", lambda e: e.tensor_tensor(out=g8(N3[nb]), in0=g8(F3), in1=bc_last(R24, 64), op=ALU.mult),
                    reads=["QF", "KF", "VSF", "R24"], writes=[("N3", nb)])

            def early_b(i):
                tok = slice(i * 128, (i + 1) * 128)
                nb = i % 2
                def trqk(e):
                    ins = None
                    for p in range(8):
                        ins = e.transpose(out=TQK[:, p, :], in_=N3[nb][:, p * 128:(p + 1) * 128], identity=IDENT)
                    return ins
                add("pe", trqk, reads=[("N3", nb), "IDENT"], writes=["B6"])
                add("dve", lambda e: e.tensor_copy(out=QT[:, :, tok], in_=TQK[:, 0:4, :]), reads=["B6"], writes=["QT", "B6"])
                add("act", lambda e: e.activation(out=KT[:, :, tok], in_=TQK[:, 4:8, :], func=AF.Copy, scale=GAINS[:, 2:3]),
                    reads=["B6", "GQK"], writes=["KT", "B6"])

            def late1a(i):
                nb = i % 2
                u3 = i % 3
                def sg(e):
                    ins = None
                    for g in range(8):
                        ins = e.matmul(Y[:, g * 64:(g + 1) * 64], lhsT=WT[:, g, :], rhs=N3[nb][:, 1024 + g * 64:1024 + (g + 1) * 64], start=True, stop=True)
                    return ins
                add("pe", sg, reads=[("N3", nb), "WT"], writes=[("P", 2)])
                add("dve", lambda e: e.tensor_tensor(out=g8(T1), in0=g8(Y), in1=bc_mid(GVBC, 8), op=ALU.mult), reads=[("P", 2), "GVBC"], writes=["T1"])
                add("pool", lambda e: e.tensor_tensor(out=g8(T1), in0=g8(T1), in1=bc_last(SGB, 64), op=ALU.add), reads=["T1", "SGB"], writes=["T1"])
                add("pool", lambda e: e.tensor_tensor(out=T1, in0=T1, in1=UF[u3], op=ALU.mult), reads=["T1", ("UF", u3)], writes=["T1"])

            def late1b(i):
                add("act", lambda e: e.activation(out=SQT, in_=T1, func=AF.Square), reads=["T1"], writes=["SQT"])
                add("dve", lambda e: e.tensor_reduce(out=SSQ8, in_=g8(SQT), axis=AX.X, op=ALU.add), reads=["SQT"], writes=["SSQ8"])

            def late1c(i):
                r = i % 2
                add("act", lambda e: e.activation(out=RT8, in_=SSQ8, func=AF.Sqrt, scale=1.0 / HD, bias=EPS), reads=["SSQ8"], writes=["RT8"])
                add("dve", lambda e: e.reciprocal(out=R8, in_=RT8), reads=["RT8"], writes=["R8"])
                add("dve", lambda e: e.tensor_tensor(out=g8(ON[r]), in0=g8(T1), in1=bc_last(R8, 64), op=ALU.mult), reads=["T1", "R8"], writes=[("ON", r)])

            def late2(i):
                r = i % 2
                tok = slice(i * 128, (i + 1) * 128)
                def trs(e):
                    ins = None
                    for p in range(4):
                        ins = e.transpose(out=TS[:, p, :], in_=ON[r][:, p * 128:(p + 1) * 128], identity=IDENT)
                    return ins
                add("pe", trs, reads=[("ON", r), "IDENT"], writes=["B7"])
                add("act", lambda e: e.activation(out=MIXT[:, 4:8, tok], in_=TS, func=AF.Copy, scale=GAINS[:, 4:5]),
                    reads=["B7"] + GAINS_R, writes=["MIXT"])

            load_x(0)
            load_x(1)
            front0(0)
            tp(0)
            load_x(2)
            for i in range(NT + 3):
                if i < NT:
                    proj(i)
                    front2(i)
                if i + 1 < NT:
                    tp(i + 1)
                if 0 <= i - 2 < NT:
                    early_b(i - 2)
                    late1a(i - 2)
                if i < NT:
                    early_a1(i)
                if i + 1 < NT:
                    front0a(i + 1)
                    if i + 3 < NT:
                        load_x(i + 3)
                if 0 <= i - 2 < NT:
                    late1b(i - 2)
                if i < NT:
                    early_a2(i)
                if i + 1 < NT:
                    front0b(i + 1)
                if 0 <= i - 2 < NT:
                    late1c(i - 2)
                if 0 <= i - 3 < NT:
                    late2(i - 3)

        def phase_b(b):
            if b == 0:
                scr_casts()
            for k in range(2):
                add("pool", lambda e, k=k: e.memset(QP[k], 0.0), writes=[("QP", k)])
            steps = []
            for p in range(4):
                for g in range(4):
                    top = 4 * g + 3
                    for kb in range(top, -1, -1):
                        steps.append((p, g, kb))
            ns = len(steps)
            Z = [ps_t[:, zb * 1024:(zb + 1) * 1024].rearrange("p (h t) -> p h t", h=2) for zb in range(3)]
            OB = bank(6)
            SSB = bank(7)

            def info(n):
                p, g, kb = steps[n]
                j = kb - 4 * g
                c0 = max(0, j) * 128
                return p, g, kb, j, c0

            def cols_of(n):
                return slice(info(n)[4], 512)

            def st_z(n):
                p, g, kb, j, c0 = info(n)
                cs = slice(c0, 512)
                qb = g % 2
                if kb == 4 * g + 3:
                    t0 = g * 512
                    add("pool", lambda e: e.tensor_copy(out=QP[qb][0:64, 0, :], in_=QT[0:64, p, t0:t0 + 512]), reads=["QT"], writes=[("QP", qb)])
                    add("pool", lambda e: e.tensor_copy(out=QP[qb][64:128, 1, :], in_=QT[64:128, p, t0:t0 + 512]), reads=["QT", ("QP", qb)], writes=[("QP", qb)])
                zb = n % 3
                def mm(e):
                    ins = None
                    for hh in range(2):
                        ins = e.matmul(Z[zb][:, hh, cs], lhsT=KT[:, p, kb * 128:(kb + 1) * 128], rhs=QP[qb][:, hh, cs], start=True, stop=True)
                    return ins
                add("pe", mm, reads=["KT", ("QP", qb)], writes=[("Z", zb)])

            def st_exp1(n):
                cs = cols_of(n)
                zb = n % 3; eb = n % 2
                add("act", lambda e: e.activation(out=E[eb][:, :, cs], in_=Z[zb][:, :, cs], func=AF.Exp), reads=[("Z", zb)], writes=[("E", eb)])

            def st_ln(n):
                p, g, kb, j, c0 = info(n)
                cs = slice(c0, 512)
                eb = n % 2; lb = n % 3
                add("act", lambda e: e.activation(out=L[lb][:, :, cs], in_=E[eb][:, :, cs], func=AF.Ln, bias=1.0), reads=[("E", eb)], writes=[("L", lb)])
                if j >= 0:
                    dc = slice(c0, c0 + 128)
                    add("dve", lambda e: e.tensor_tensor(out=L[lb][:, :, dc], in0=L[lb][:, :, dc], in1=bc_mid(MASK, 2), op=ALU.mult),
                        reads=[("L", lb), "MASK"], writes=[("L", lb)])

            def st_carry(n):
                p, g, kb, j, c0 = info(n)
                cs = slice(c0, 512)
                lb = n % 3
                if kb >= 2:
                    if kb == 4 * g + 3:
                        add("dve", lambda e: e.memset(LS, 0.0), writes=["LS", "LS1"])
                    add("pool", lambda e: e.tensor_tensor(out=LS[:, :, cs], in0=LS[:, :, cs], in1=L[lb][:, :, cs], op=ALU.add),
                        reads=["LS", ("L", lb)], writes=["LS"])

            def st_cast(n):
                p, g, kb, j, c0 = info(n)
                if kb >= 2:
                    ncs = cols_of(n + 2)
                    sb = n % 2
                    add("dve", lambda e: e.tensor_copy(out=LSB[sb][:, :, ncs], in_=LS[:, :, ncs]), reads=["LS", "LS1"], writes=[("LSB", sb)])

            def st_pe2(n):
                p, g, kb, j, c0 = info(n)
                cs = slice(c0, 512)
                zb = n % 3; lb = n % 3; sb = n % 2
                plb = (n - 1) % 3
                m = 4 * g + 3 - kb
                pcs = cols_of(n - 1) if m >= 1 else None
                def mm(e):
                    ins = None
                    for hh in range(2):
                        if m >= 2:
                            e.matmul(Z[zb][:, hh, cs], lhsT=NEGONES, rhs=LSB[sb][:, hh, cs], start=False, stop=False, skip_group_check=True)
                        if m >= 1:
                            e.matmul(Z[zb][:, hh, pcs], lhsT=NEGONES, rhs=L[plb][:, hh, pcs], start=False, stop=False, skip_group_check=True)
                        ins = e.matmul(Z[zb][:, hh, cs], lhsT=NEGTRI, rhs=L[lb][:, hh, cs], start=False, stop=True, skip_group_check=True)
                    return ins
                rd = [("L", lb), "NEGTRI", "NEGONES"]
                if m >= 1:
                    rd.append(("L", plb))
                if m >= 2:
                    rd.append(("LSB", sb))
                add("pe", mm, reads=rd, writes=[("Z", zb)])
                st_carry(n)

            def st_exp3(n):
                p, g, kb, j, c0 = info(n)
                cs = slice(c0, 512)
                zb = n % 3; ab = n % 2
                add("act", lambda e: e.activation(out=A[ab][:, :, cs], in_=Z[zb][:, :, cs], func=AF.Exp), reads=[("Z", zb)], writes=[("A", ab)])
                if j >= 0:
                    dc = slice(c0, c0 + 128)
                    add("dve", lambda e: e.tensor_tensor(out=A[ab][:, :, dc], in0=A[ab][:, :, dc], in1=bc_mid(MASK, 2), op=ALU.mult),
                        reads=[("A", ab), "MASK"], writes=[("A", ab)])

            def st_av(n):
                p, g, kb, j, c0 = info(n)
                cs = slice(c0, 512)
                ab = n % 2
                first = (kb == 4 * g + 3)
                last = (kb == 0)
                def mm(e):
                    ins = None
                    for hh in range(2):
                        ins = e.matmul(OB[:, cs], lhsT=VPAD[:, kb, p, hh, :], rhs=A[ab][:, hh, cs], start=(first and hh == 0),
                                       stop=(last and hh == 1), skip_group_check=True)
                    return ins
                add("pe", mm, reads=["VPAD", ("A", ab)], writes=["OB"])
                if last:
                    t0 = g * 512
                    add("dve", lambda e: e.tensor_copy(out=OSB, in_=OB), reads=["OB"], writes=["OSB"])
                    add("pool", lambda e: e.tensor_tensor(out=SQB, in0=OSB, in1=OSB, op=ALU.mult), reads=["OSB"], writes=["SQB"])
                    pending.setdefault(n + 1, []).append(
                        lambda: add("pe", lambda e: e.matmul(SSB, lhsT=BDONES, rhs=SQB, start=True, stop=True), reads=["SQB", "BDONES"], writes=["SSB"]))
                    def act_part():
                        add("act", lambda e: e.activation(out=R, in_=SSB, func=AF.Ln, scale=1.0 / HD, bias=EPS), reads=["SSB"], writes=["R"])
                        add("act", lambda e: e.activation(out=R, in_=R, func=AF.Exp, scale=-0.5), reads=["R"], writes=["R"])
                    pending.setdefault(n + 2, []).append(act_part)
                    pending.setdefault(n + 3, []).append(
                        lambda: add("dve", lambda e: e.scalar_tensor_tensor(out=MIXT[:, p, t0:t0 + 512], in0=OSB, scalar=GAINS[:, 3:4], in1=R,
                                                                            op0=ALU.mult, op1=ALU.mult), reads=["OSB", "R"] + GAINS_R, writes=["MIXT"]))

            pending = {}

            for n0 in range(3):
                st_z(n0)
            st_exp1(0)
            for n in range(-1, ns):
                if n + 1 < ns:
                    st_ln(n + 1)
                    st_pe2(n + 1)
                if n >= 0:
                    st_exp3(n)
                    for pc in pending.pop(n, []):
                        pc()
                    st_av(n)
                    if n + 3 < ns:
                        st_z(n + 3)
                if n + 2 < ns:
                    st_exp1(n + 2)
                if n + 1 < ns:
                    st_cast(n + 1)
            for kk in sorted(pending):
                for pc in pending[kk]:
                    pc()

        def phase_c(b):
            HO = [bank(0, 2), bank(2, 2)]
            TPH = bank_bf(4).rearrange("p (c t) -> p c t", c=8)
            FB = [bank(4), bank(5)]
            O2 = [bank(6)[:, 0:256], bank(7)[:, 0:256]]
            s1v = scr1.rearrange("(c p) f -> p c f", p=128)
            s2v = scr2.rearrange("(c p) m -> p c m", p=128)

            def load_xr(T, sub):
                r = sub % 2
                tok0 = T * 512 + sub * 128
                add("sp", lambda e: e.dma_start(out=XR[r], in_=x[b, tok0:tok0 + 128, :]), writes=[("XR", r)], dma="XR%d" % r)

            def load_w1(fg):
                r = fg % 3
                add("act", lambda e: e.dma_start(out=W1[r], in_=s1v[:, :, fg * 512:(fg + 1) * 512]), reads=SCR1_R, writes=[("W1", r)], dma="W1%d" % r)

            def load_w2(q):
                r = q % 2
                add("sp", lambda e: e.dma_start(out=W2[r], in_=s2v[:, :, q * 256:(q + 1) * 256]), reads=SCR2_R, writes=[("W2", r)], dma="W2%d" % r)

            def c1_mm(T, sub):
                hb = sub % 2
                tok = slice(T * 512 + sub * 128, T * 512 + (sub + 1) * 128)
                def mm(e):
                    ins = None
                    for half in range(2):
                        for c in range(8):
                            ins = e.matmul(HO[hb][:, half * 512:(half + 1) * 512], lhsT=MIXT[:, c, tok], rhs=WOUT[:, c, half * 512:(half + 1) * 512],
                                           start=(c == 0), stop=(c == 7))
                    return ins
                add("pe", mm, reads=["MIXT"] + WOUT_R, writes=[("HO", hb)])

            def c1_mid(T, sub):
                hb = sub % 2
                r = sub % 2
                tb = T % 2
                add("dve", lambda e: e.tensor_tensor(out=H1[tb][:, sub, :], in0=HO[hb], in1=XR[r], op=ALU.add),
                    reads=[("HO", hb), ("XR", r)], writes=[("H1", tb, sub)])
                if sub + 2 < 4:
                    load_xr(T, sub + 2)
                add("act", lambda e: e.activation(out=H1B, in_=H1[tb][:, sub, :], func=AF.Copy), reads=[("H1", tb, sub)], writes=["H1B"])
                add("act", lambda e: e.activation(out=JUNKC, in_=H1[tb][:, sub, :], func=AF.Square, accum_out=SS2[:, sub:sub + 1]),
                    reads=[("H1", tb, sub)], writes=[("SS2", sub)])
                add("act", lambda e: e.activation(out=RT2[:, sub:sub + 1], in_=SS2[:, sub:sub + 1], func=AF.Sqrt, scale=1.0 / D, bias=EPS),
                    reads=[("SS2", sub)], writes=[("RT2", sub)])
                add("dve", lambda e: e.reciprocal(out=R2[:, sub:sub + 1], in_=RT2[:, sub:sub + 1]), reads=[("RT2", sub)], writes=[("R2", sub)])
                add("dve", lambda e: e.tensor_tensor(out=R2SQ[tb][:, sub:sub + 1], in0=R2[:, sub:sub + 1], in1=R2[:, sub:sub + 1], op=ALU.mult),
                    reads=[("R2", sub)], writes=[("R2SQ", tb, sub)])

            def c1_tr(T, sub):
                tb = T % 2
                def tr(e):
                    ins = None
                    for c in range(8):
                        ins = e.transpose(out=TPH[:, c, :], in_=H1B[:, c * 128:(c + 1) * 128], identity=IDENT)
                    return ins
                add("pe", tr, reads=["H1B", "IDENT"], writes=[("FB", 0)])
                add("dve", lambda e: e.tensor_tensor(out=H1T[tb][:, :, sub * 128:(sub + 1) * 128], in0=TPH, in1=bc_last(G2T, 128), op=ALU.mult),
                    reads=[("FB", 0), "G2T"], writes=[("H1T", tb)])

            def c1_pieces(T):
                return {
                    0: [lambda: (load_xr(T, 0), load_xr(T, 1))],
                    1: [lambda: c1_mm(T, 0)],
                    2: [lambda: c1_mm(T, 1)],
                    3: [lambda: c1_mid(T, 0)],
                    4: [lambda: c1_mm(T, 2)],
                    5: [lambda: c1_tr(T, 0), lambda: c1_mid(T, 1)],
                    6: [lambda: c1_mm(T, 3)],
                    7: [lambda: c1_tr(T, 1), lambda: c1_mid(T, 2)],
                    9: [lambda: c1_tr(T, 2), lambda: c1_mid(T, 3)],
                    11: [lambda: c1_tr(T, 3)],
                }

            def c2(T):
                tb = T % 2
                for fg in range(8):
                    r = fg % 3
                    for fj in range(4):
                        fc = fg * 4 + fj
                        fb = fc % 2
                        def mm(e, r=r, fj=fj, fb=fb):
                            ins = None
                            for c in range(8):
                                ins = e.matmul(FB[fb], lhsT=W1[r][:, c, fj * 128:(fj + 1) * 128], rhs=H1T[tb][:, c, :], start=(c == 0), stop=(c == 7))
                            return ins
                        add("pe", mm, reads=[("W1", r), ("H1T", tb)], writes=[("FB", fb)])
                        add("act", lambda e, fb=fb: e.activation(out=RL[fb], in_=FB[fb], func=AF.Relu), reads=[("FB", fb)], writes=[("RL", fb)])
                        add("dve", lambda e, fb=fb, fc=fc: e.tensor_tensor(out=AFF[:, fc, :], in0=RL[fb], in1=RL[fb], op=ALU.mult),
                            reads=[("RL", fb)], writes=["AFF"])
                    if fg + 3 < 8:
                        load_w1(fg + 3)

            def c3(T, pieces):
                tb = T % 2
                k = 0
                for q in range(4):
                    r = q % 2
                    for sub in range(4):
                        ob = (q * 4 + sub) % 2
                        def mm(e, r=r, sub=sub, ob=ob):
                            ins = None
                            for fc in range(32):
                                ins = e.matmul(O2[ob], lhsT=AFF[:, fc, sub * 128:(sub + 1) * 128], rhs=W2[r][:, fc, :], start=(fc == 0), stop=(fc == 31))
                            return ins
                        add("pe", mm, reads=["AFF", ("W2", r)], writes=[("O2", ob)])
                        qs = slice(q * 256, (q + 1) * 256)
                        add("dve", lambda e, sub=sub, ob=ob, qs=qs: e.scalar_tensor_tensor(out=H1[tb][:, sub, qs], in0=O2[ob], scalar=R2SQ[tb][:, sub:sub + 1],
                                                                                         in1=H1[tb][:, sub, qs], op0=ALU.mult, op1=ALU.add),
                            reads=[("O2", ob), ("R2SQ", tb, sub), ("H1", tb, sub)], writes=[("H1", tb, sub), ("O2", ob)])
                        for pc in pieces.get(k, []):
                            pc()
                        k += 1
                    if q + 2 < 4:
                        load_w2(q + 2)
                    elif T + 1 < 4:
                        load_w2(q - 2)
                    if q == 1 and T == 3 and b + 1 < NB:
                        win_load(b + 1, alias=True)
                for sub in range(4):
                    tok0 = T * 512 + sub * 128
                    add("sp", lambda e, sub=sub, tok0=tok0: e.dma_start(out=out[b, tok0:tok0 + 128, :], in_=H1[tb][:, sub, :]),
                        reads=[("H1", tb, sub)], writes=["OUT"], dma="out%d" % sub)

            load_w1(0)
            load_w1(1)
            load_w1(2)
            p0 = c1_pieces(0)
            for kk in sorted(p0):
                for pc in p0[kk]:
                    pc()
            load_w2(0)
            load_w2(1)
            for T in range(4):
                c2(T)
                if T + 1 < 4:
                    load_w1(0)
                    load_w1(1)
                    load_w1(2)
                c3(T, c1_pieces(T + 1) if T + 1 < 4 else {})

        setup()
        win_load(0)
        for b in range(NB):
            phase_a(b)
            SC.barrier()
            phase_b(b)
            SC.barrier()
            phase_c(b)
            SC.barrier()

        eng_sems = {e: es.enter_context(nc.semaphore("s_" + e)) for e in ENGS}
        dma_sems = {s: es.enter_context(nc.semaphore("d_" + s)) for s in SC.dma_slots()}
        SC.assign_tokens(eng_sems, dma_sems)

        block = es.enter_context(nc.Block())

        @block.tensor
        def _(e):
            SC.emit("pe", e)

        @block.scalar
        def _(e):
            SC.emit("act", e)

        @block.vector
        def _(e):
            SC.emit("dve", e)

        @block.gpsimd
        def _(e):
            SC.emit("pool", e)

        @block.sync
        def _(e):
            SC.emit("sp", e)

    return nc


_NC = None


def kernel(x, norm1_g, w_in, q_norm_g, k_norm_g, sg_v_norm_g, sg_w, sg_b,
           sb_out_norm_g, sg_out_norm_g, w_out, norm2_g, w_ff1, w_ff2):
    global _NC
    if _NC is None:
        _NC = build_nc()
    f = lambda a: np.ascontiguousarray(np.asarray(a, dtype=np.float32))
    x = f(x)
    shared = {
        "norm1_g": f(norm1_g), "w_in": f(w_in), "q_norm_g": f(q_norm_g), "k_norm_g": f(k_norm_g),
        "sg_v_norm_g": f(sg_v_norm_g), "sg_w": f(sg_w), "sg_b": f(sg_b), "sb_out_norm_g": f(sb_out_norm_g),
        "sg_out_norm_g": f(sg_out_norm_g), "w_out": f(w_out), "norm2_g": f(norm2_g), "w_ff1": f(w_ff1), "w_ff2": f(w_ff2),
    }
    in_maps = []
    for c in range(NCORES):
        m = dict(shared)
        m["x"] = np.ascontiguousarray(x[c * NB:(c + 1) * NB])
        in_maps.append(m)
    res = run_bass_kernel_spmd(_NC, in_maps, core_ids=list(range(NCORES)))
    outs = [np.asarray(res.results[c]["out"], dtype=np.float32) for c in range(NCORES)]
    return np.concatenate(outs, axis=0)
```
